# Optimizing a Trainium2 kernel written in Bass

```python
import math
import jax
import jax.numpy as jnp
from jax import lax
import numpy as np

D_MODEL = 1024
BATCH = 8
SEQ = 2048
DEPTH = 4
DEC_BATCH = 128
DEC_SEQ = 8
PAST_LEN = 16384
PAGE_SIZE = 128

N_EVEN = (DEPTH + 1) // 2
N_ODD = DEPTH // 2
H_A = 4
DK_A = 128
DV_A = 128
W_A = H_A * DV_A
H_B = 4
W_B = D_MODEL // 2
BLK_B = W_B // H_B
LRU_C = 8.0
CONV_W = 4
D_INNER_C = 2 * D_MODEL
P_C = 64
H_C = D_INNER_C // P_C
N_C = 128
G_C = 4
HPG_C = H_C // G_C
CONV_DIM_C = D_INNER_C + 2 * G_C * N_C
IN_C = D_INNER_C + CONV_DIM_C + H_C
IN_EVEN = 2 * H_A * DK_A + 2 * W_A + 2 * W_B
D_FF = -(-8 * D_MODEL // (3 * 256)) * 256
PLE_DIM = 256
CHUNK = 64
EPS = 1e-6
F32 = jnp.float32

kernel_name = 'hgrn2_rglru_mamba2_hybrid_step'


def _rmsnorm(x, g):
    xf = x.astype(F32)
    y = xf * lax.rsqrt(jnp.mean(xf * xf, axis=-1, keepdims=True) + EPS)
    return (y * g.astype(F32)).astype(x.dtype)


def _chunk_len(L):
    return L if L <= CHUNK else math.gcd(L, CHUNK)


def _to_chunks(t, c):
    B, L = t.shape[:2]
    return jnp.moveaxis(t.reshape((B, L // c, c) + t.shape[2:]), 1, 0)


def _from_chunks(t):
    nc, B, c = t.shape[:3]
    return jnp.moveaxis(t, 0, 1).reshape((B, nc * c) + t.shape[3:])


def _causal_conv(u, buf, w, b):
    L = u.shape[1]
    full = jnp.concatenate([buf.astype(u.dtype), u], axis=1)
    out = full[:, 0:L] * w[0]
    for k in range(1, CONV_W):
        out = out + full[:, k:k + L] * w[k]
    return out + b, full[:, L:]


def _hgrn2_scan(q, logf, k, v, S0):
    c = _chunk_len(q.shape[1])
    causal = jnp.tril(jnp.ones((c, c), dtype=bool))[None, :, :, None, None]

    def step(S, inp):
        qc, lfc, kc, vc = inp
        b = jnp.cumsum(lfc, axis=1)
        o = jnp.einsum('bthd,bhde->bthe', qc * jnp.exp(b), S)
        decay = jnp.exp(jnp.where(causal, b[:, :, None] - b[:, None, :], -jnp.inf))
        scores = jnp.einsum('bthd,bshd,btshd->bhts', qc, kc, decay)
        o = o + jnp.einsum('bhts,bshe->bthe', scores, vc)
        b_last = b[:, -1]
        k_dec = kc * jnp.exp(b_last[:, None] - b)
        S = jnp.exp(b_last)[..., None] * S + jnp.einsum('bshd,bshe->bhde', k_dec, vc)
        return S, o

    S, o = lax.scan(step, S0, (_to_chunks(q, c), _to_chunks(logf, c), _to_chunks(k, c), _to_chunks(v, c)))
    return _from_chunks(o), S


def _ssd_scan(xs, dt, log_a, Bm, Cm, S0):
    c = _chunk_len(xs.shape[1])
    causal = jnp.tril(jnp.ones((c, c), dtype=bool))[None, :, :, None]

    def step(S, inp):
        xc, dtc, lac, Bc, Cc = inp
        Bh = jnp.repeat(Bc, HPG_C, axis=2)
        Ch = jnp.repeat(Cc, HPG_C, axis=2)
        cum = jnp.cumsum(lac, axis=1)
        o = jnp.einsum('bthn,bhpn->bthp', Ch, S) * jnp.exp(cum)[..., None]
        decay = jnp.exp(jnp.where(causal, cum[:, :, None] - cum[:, None, :], -jnp.inf))
        scores = jnp.einsum('bthn,bshn->btsh', Ch, Bh) * decay * dtc[:, None]
        o = o + jnp.einsum('btsh,bshp->bthp', scores, xc)
        last = cum[:, -1]
        w = jnp.exp(last[:, None] - cum) * dtc
        S = jnp.exp(last)[..., None, None] * S + jnp.einsum('bsh,bshp,bshn->bhpn', w, xc, Bh)
        return S, o

    S, o = lax.scan(step, S0, (_to_chunks(xs, c), _to_chunks(dt, c), _to_chunks(log_a, c),
                               _to_chunks(Bm, c), _to_chunks(Cm, c)))
    return _from_chunks(o), S


def _rglru(u, w_a, b_a, w_x, b_x, lam, h0, fresh):
    B, L, _ = u.shape
    ub = u.reshape(B, L, H_B, BLK_B)
    r = jax.nn.sigmoid(jnp.einsum('blhi,hij->blhj', ub, w_a) + b_a).reshape(B, L, W_B)
    gi = jax.nn.sigmoid(jnp.einsum('blhi,hij->blhj', ub, w_x) + b_x).reshape(B, L, W_B)
    log_a = -LRU_C * r * jax.nn.softplus(-lam.astype(F32))
    a = jnp.exp(log_a)
    mult = jnp.sqrt(-jnp.expm1(2.0 * log_a))
    if fresh:
        mult = mult.at[:, 0].set(1.0)
    bt = mult * (gi * u)
    bt = bt.at[:, 0].add(a[:, 0] * h0)

    def combine(lhs, rhs):
        a1, b1 = lhs
        a2, b2 = rhs
        return a1 * a2, a2 * b1 + b2

    _, h = lax.associative_scan(combine, (a, bt), axis=1)
    return h, h[:, -1]


def _even_mixer(xn, S0, h0, cbuf0, fresh, lb, W, j):
    B, L, _ = xn.shape
    hk = H_A * DK_A
    proj = (xn @ W['w_even_in'][j]).astype(F32)
    q, fz, iv, g, yb, ub = jnp.split(
        proj, [hk, 2 * hk, 2 * hk + W_A, 2 * hk + 2 * W_A, 2 * hk + 2 * W_A + W_B], axis=-1)
    lb = lb.reshape(H_A, DK_A)
    fz = fz.reshape(B, L, H_A, DK_A)
    logf = jnp.logaddexp(jnp.log(lb), jnp.log1p(-lb) + jax.nn.log_sigmoid(fz))
    k = (1.0 - lb) * jax.nn.sigmoid(-fz)
    q = jax.nn.silu(q).reshape(B, L, H_A, DK_A)
    o_a, S = _hgrn2_scan(q, logf, k, iv.reshape(B, L, H_A, DV_A), S0.astype(F32))
    o_a = o_a * lax.rsqrt(jnp.mean(o_a * o_a, axis=-1, keepdims=True) + EPS)
    o_a = o_a.reshape(B, L, W_A) * W['hgrn_gnorm'][j].astype(F32) * jax.nn.silu(g)
    u, cbuf = _causal_conv(ub, cbuf0.astype(F32), W['lru_conv_w'][j], W['lru_conv_b'][j])
    h, h_last = _rglru(u, W['lru_wa'][j], W['lru_ba'][j], W['lru_wx'][j], W['lru_bx'][j],
                       W['lru_lam'][j], h0.astype(F32), fresh)
    o_b = jax.nn.gelu(yb, approximate=True) * h
    out = jnp.concatenate([o_a, o_b], axis=-1).astype(xn.dtype) @ W['w_even_out'][j]
    return out, S, h_last, cbuf


def _odd_mixer(xn, S0, cbuf0, W, j):
    B, L, _ = xn.shape
    proj = (xn @ W['ssm_in'][j]).astype(F32)
    z = proj[..., :D_INNER_C]
    xbc = proj[..., D_INNER_C:D_INNER_C + CONV_DIM_C]
    dt = proj[..., D_INNER_C + CONV_DIM_C:]
    xbc, cbuf = _causal_conv(xbc, cbuf0.astype(F32), W['ssm_conv_w'][j], W['ssm_conv_b'][j])
    xbc = jax.nn.silu(xbc)
    xs = xbc[..., :D_INNER_C].reshape(B, L, H_C, P_C)
    Bm = xbc[..., D_INNER_C:D_INNER_C + G_C * N_C].reshape(B, L, G_C, N_C)
    Cm = xbc[..., D_INNER_C + G_C * N_C:].reshape(B, L, G_C, N_C)
    dt = jax.nn.softplus(dt + W['ssm_dt_bias'][j].astype(F32))
    A = -jnp.exp(W['ssm_a_log'][j].astype(F32))
    y, S = _ssd_scan(xs, dt, dt * A, Bm, Cm, S0.astype(F32))
    y = y + W['ssm_d'][j].astype(F32)[:, None] * xs
    y = y.reshape(B, L, D_INNER_C) * jax.nn.silu(z)
    yg = y.reshape(B, L, G_C, D_INNER_C // G_C)
    yg = yg * lax.rsqrt(jnp.mean(yg * yg, axis=-1, keepdims=True) + EPS)
    y = yg.reshape(B, L, D_INNER_C) * W['ssm_gnorm'][j].astype(F32)
    out = y.astype(xn.dtype) @ W['ssm_out'][j]
    return out, S, cbuf


def _swiglu(x, w1, w3, w2):
    return (jax.nn.silu(x @ w1) * (x @ w3)) @ w2


def _ple(x, p_i, w_up, w_gate, g):
    gate = jax.nn.sigmoid((x @ w_gate).astype(F32))
    e = (p_i @ w_up).astype(F32)
    return _rmsnorm(gate * e, g).astype(x.dtype)


def _trunk(x, p, st_hgrn, st_lru_h, st_lru_conv, st_ssm, st_ssm_conv, fresh, W):
    lb_all = jnp.cumsum(jax.nn.softmax(W['hgrn_lb'].astype(F32), axis=0), axis=0)
    lb_all = lb_all - lb_all[0]
    hg, lh, lc, ss, sc = [], [], [], [], []
    for i in range(DEPTH):
        j = i // 2
        xn = _rmsnorm(x, W['g_mix'][i])
        if i % 2 == 0:
            mix, s_a, s_h, s_c = _even_mixer(xn, st_hgrn[j], st_lru_h[j], st_lru_conv[j], fresh, lb_all[j], W, j)
            hg.append(s_a)
            lh.append(s_h)
            lc.append(s_c)
        else:
            mix, s_s, s_c = _odd_mixer(xn, st_ssm[j], st_ssm_conv[j], W, j)
            ss.append(s_s)
            sc.append(s_c)
        x = x + mix
        x = x + _swiglu(_rmsnorm(x, W['g_ffn'][i]), W['ffn_w1'][i], W['ffn_w3'][i], W['ffn_w2'][i])
        x = x + _ple(x, p[i], W['ple_up'][i], W['ple_gate'][i], W['g_ple'][i])
    y = _rmsnorm(x, W['g_final'])
    return (y, jnp.stack(hg).astype(st_hgrn.dtype), jnp.stack(lh).astype(st_lru_h.dtype),
            jnp.stack(lc).astype(st_lru_conv.dtype), jnp.stack(ss).astype(st_ssm.dtype),
            jnp.stack(sc).astype(st_ssm_conv.dtype))


def setup_inputs(seed: int = 0) -> dict:
    key = jax.random.key(seed)
    ks = iter(jax.random.split(key, 64))
    D = D_MODEL

    def nrm(shape, scale):
        return jax.random.normal(next(ks), shape, F32) * scale

    def unif(shape, lo, hi):
        return jax.random.uniform(next(ks), shape, F32, lo, hi)

    a0 = unif((N_EVEN, W_B), 0.9, 0.999) ** (1.0 / LRU_C)
    dt0 = jnp.exp(unif((N_ODD, H_C), math.log(1e-3), math.log(1e-1)))
    inputs = {}
    inputs['x_prompt'] = nrm((BATCH, SEQ, D), 1.0)
    inputs['x_sample'] = nrm((DEC_BATCH, DEC_SEQ, D), 1.0)
    inputs['state_hgrn'] = nrm((N_EVEN, DEC_BATCH, H_A, DK_A, DV_A), 0.5)
    inputs['state_lru_h'] = nrm((N_EVEN, DEC_BATCH, W_B), 0.5)
    inputs['state_lru_conv'] = nrm((N_EVEN, DEC_BATCH, CONV_W - 1, W_B), 1.0)
    inputs['state_ssm'] = nrm((N_ODD, DEC_BATCH, H_C, P_C, N_C), 0.1)
    inputs['state_ssm_conv'] = nrm((N_ODD, DEC_BATCH, CONV_W - 1, CONV_DIM_C), 1.0)
    inputs['p_prompt'] = nrm((DEPTH, BATCH, SEQ, PLE_DIM), 1.0)
    inputs['p_sample'] = nrm((DEPTH, DEC_BATCH, DEC_SEQ, PLE_DIM), 1.0)
    inputs['g_mix'] = 1.0 + nrm((DEPTH, D), 0.05)
    inputs['g_ffn'] = 1.0 + nrm((DEPTH, D), 0.05)
    inputs['g_ple'] = 1.0 + nrm((DEPTH, D), 0.05)
    inputs['g_final'] = 1.0 + nrm((D,), 0.05)
    inputs['w_even_in'] = nrm((N_EVEN, D, IN_EVEN), D ** -0.5)
    inputs['hgrn_lb'] = nrm((N_EVEN, H_A * DK_A), 0.1)
    inputs['hgrn_gnorm'] = 1.0 + nrm((N_EVEN, W_A), 0.05)
    inputs['lru_conv_w'] = nrm((N_EVEN, CONV_W, W_B), CONV_W ** -0.5)
    inputs['lru_conv_b'] = nrm((N_EVEN, W_B), 0.02)
    inputs['lru_wa'] = nrm((N_EVEN, H_B, BLK_B, BLK_B), BLK_B ** -0.5)
    inputs['lru_ba'] = nrm((N_EVEN, H_B, BLK_B), 0.02)
    inputs['lru_wx'] = nrm((N_EVEN, H_B, BLK_B, BLK_B), BLK_B ** -0.5)
    inputs['lru_bx'] = nrm((N_EVEN, H_B, BLK_B), 0.02)
    inputs['lru_lam'] = jnp.log(a0) - jnp.log1p(-a0)
    inputs['w_even_out'] = nrm((N_EVEN, W_A + W_B, D), (W_A + W_B) ** -0.5)
    inputs['ssm_in'] = nrm((N_ODD, D, IN_C), D ** -0.5)
    inputs['ssm_conv_w'] = nrm((N_ODD, CONV_W, CONV_DIM_C), CONV_W ** -0.5)
    inputs['ssm_conv_b'] = nrm((N_ODD, CONV_DIM_C), 0.02)
    inputs['ssm_dt_bias'] = dt0 + jnp.log(-jnp.expm1(-dt0))
    inputs['ssm_a_log'] = jnp.log(unif((N_ODD, H_C), 1.0, 16.0))
    inputs['ssm_d'] = 1.0 + nrm((N_ODD, H_C), 0.1)
    inputs['ssm_gnorm'] = 1.0 + nrm((N_ODD, D_INNER_C), 0.05)
    inputs['ssm_out'] = nrm((N_ODD, D_INNER_C, D), D_INNER_C ** -0.5)
    inputs['ffn_w1'] = nrm((DEPTH, D, D_FF), D ** -0.5)
    inputs['ffn_w3'] = nrm((DEPTH, D, D_FF), D ** -0.5)
    inputs['ffn_w2'] = nrm((DEPTH, D_FF, D), D_FF ** -0.5)
    inputs['ple_up'] = nrm((DEPTH, PLE_DIM, D), PLE_DIM ** -0.5)
    inputs['ple_gate'] = nrm((DEPTH, D, D), D ** -0.5)
    return inputs


def reference(x_prompt, x_sample, state_hgrn, state_lru_h, state_lru_conv, state_ssm, state_ssm_conv,
              p_prompt, p_sample, g_mix, g_ffn, g_ple, g_final, w_even_in, hgrn_lb, hgrn_gnorm,
              lru_conv_w, lru_conv_b, lru_wa, lru_ba, lru_wx, lru_bx, lru_lam, w_even_out,
              ssm_in, ssm_conv_w, ssm_conv_b, ssm_dt_bias, ssm_a_log, ssm_d, ssm_gnorm, ssm_out,
              ffn_w1, ffn_w3, ffn_w2, ple_up, ple_gate):
    W = dict(g_mix=g_mix, g_ffn=g_ffn, g_ple=g_ple, g_final=g_final, w_even_in=w_even_in,
             hgrn_lb=hgrn_lb, hgrn_gnorm=hgrn_gnorm, lru_conv_w=lru_conv_w, lru_conv_b=lru_conv_b,
             lru_wa=lru_wa, lru_ba=lru_ba, lru_wx=lru_wx, lru_bx=lru_bx, lru_lam=lru_lam,
             w_even_out=w_even_out, ssm_in=ssm_in, ssm_conv_w=ssm_conv_w, ssm_conv_b=ssm_conv_b,
             ssm_dt_bias=ssm_dt_bias, ssm_a_log=ssm_a_log, ssm_d=ssm_d, ssm_gnorm=ssm_gnorm,
             ssm_out=ssm_out, ffn_w1=ffn_w1, ffn_w3=ffn_w3, ffn_w2=ffn_w2, ple_up=ple_up,
             ple_gate=ple_gate)
    bp = x_prompt.shape[0]
    dtp = x_prompt.dtype
    z_hgrn = jnp.zeros((N_EVEN, bp, H_A, DK_A, DV_A), dtp)
    z_lru_h = jnp.zeros((N_EVEN, bp, W_B), dtp)
    z_lru_conv = jnp.zeros((N_EVEN, bp, CONV_W - 1, W_B), dtp)
    z_ssm = jnp.zeros((N_ODD, bp, H_C, P_C, N_C), dtp)
    z_ssm_conv = jnp.zeros((N_ODD, bp, CONV_W - 1, CONV_DIM_C), dtp)
    y_prompt, hg_p, lh_p, lc_p, ss_p, sc_p = _trunk(
        x_prompt, p_prompt, z_hgrn, z_lru_h, z_lru_conv, z_ssm, z_ssm_conv, True, W)
    y_sample, hg_s, lh_s, lc_s, ss_s, sc_s = _trunk(
        x_sample, p_sample, state_hgrn, state_lru_h, state_lru_conv, state_ssm, state_ssm_conv, False, W)
    return (y_prompt, y_sample, hg_p, hg_s, lh_p, lh_s, lc_p, lc_s, ss_p, ss_s, sc_p, sc_s)
```

```python
import numpy as np
from contextlib import ExitStack
import concourse.bass as bass
import concourse.mybir as mybir
from concourse.bass_utils import run_bass_kernel_spmd

F32 = mybir.dt.float32
BF16 = mybir.dt.bfloat16
AF = mybir.ActivationFunctionType
OP = mybir.AluOpType

ENGS = ("pe", "act", "dve", "pool", "sp")
SAME_ENGINE_SYNC = False
PEN = "pool"
SELF_WINDOW = 2000

NCORES = 8
D = 1024
KD = 8
SEQ = 2048
NSEQ_S = 16
LS = 8
DEPTH = 4
DFF = 2816
KFF = 22
PLE = 256
EPS = 1e-6
NW = 640


class Sch:
    def __init__(self, nc, stack):
        self.nc = nc
        self.stack = stack
        self.q = {e: [] for e in ENGS}
        self.cnt = {e: 0 for e in ENGS}
        self.sem = {e: stack.enter_context(nc.semaphore("c_" + e)) for e in ENGS}
        self.seen = {e: {} for e in ENGS}
        self.lastw = {}
        self.readers = {}
        self.dsem = {}
        self.pool_i = 0
        self.out_tokens = []
        self.cyc = {e: 0 for e in ENGS}
        self.stamp = {e: {} for e in ENGS}
        self.n_self = 0

    def _need(self, eng, tok, waits, fence=False):
        if tok is None:
            return
        sem, val, src = tok
        if src == eng and not (SAME_ENGINE_SYNC or fence):
            if eng == "pe" or (eng != "pool" and self.cyc[eng] - self.stamp[eng].get(val, -10 ** 9) >= SELF_WINDOW):
                return
            self.n_self += 1
        key = id(sem)
        if self.seen[eng].get(key, 0) >= val:
            return
        cur = waits.get(key)
        if cur is None or cur[1] < val:
            waits[key] = (sem, val)

    def _deps(self, eng, reads, writes, fence=False):
        waits = {}
        for r in reads:
            self._need(eng, self.lastw.get(r), waits, fence)
        for w in writes:
            self._need(eng, self.lastw.get(w), waits)
            for t in self.readers.get(w, ()):
                self._need(eng, t, waits)
        for key, (sem, val) in waits.items():
            self.seen[eng][key] = val
        return list(waits.values())

    def _commit(self, tok, reads, writes):
        for r in reads:
            self.readers.setdefault(r, []).append(tok)
        for w in writes:
            self.lastw[w] = tok
            self.readers[w] = []

    def op(self, eng, fn, reads=(), writes=(), fence=False, cost=64):
        waits = self._deps(eng, reads, writes, fence)
        self.cnt[eng] += 1
        val = self.cnt[eng]
        self.cyc[eng] += max(64, cost)
        self.stamp[eng][val] = self.cyc[eng]
        sem = self.sem[eng]

        def emit(e, waits=waits, fn=fn, sem=sem):
            for (s, v) in waits:
                e.wait_ge(s, v)
            fn(e).then_inc(sem, 1)

        self.q[eng].append(emit)
        tok = (sem, val, eng)
        self._commit(tok, reads, writes)
        return tok

    NPOOL = 24

    def _get_dsem(self, res):
        if res is None or (isinstance(res, tuple) and res[0] == "q"):
            res = ("pool", self.pool_i % self.NPOOL)
            self.pool_i += 1
        d = self.dsem.get(res)
        if d is None:
            sem = self.stack.enter_context(self.nc.semaphore("d%d" % len(self.dsem)))
            d = [sem, 0]
            self.dsem[res] = d
        return d

    def dma(self, eng, pairs, reads=(), writes=(), semres=None, is_output=False):
        waits = self._deps(eng, reads, writes)
        d = self._get_dsem(semres)
        if d[1] > 0 and self.seen[eng].get(id(d[0]), 0) < d[1]:
            self.seen[eng][id(d[0])] = d[1]
            waits = [w for w in waits if w[0] is not d[0]] + [(d[0], d[1])]
        d[1] += 16 * len(pairs)
        sem, val = d[0], d[1]

        def emit(e, waits=waits, pairs=pairs, sem=sem):
            for (s, v) in waits:
                e.wait_ge(s, v)
            for (o, i) in pairs:
                e.dma_start(out=o, in_=i).then_inc(sem, 16)

        self.q[eng].append(emit)
        tok = (sem, val, None)
        self._commit(tok, reads, writes)
        if is_output:
            self.out_tokens.append(tok)
        return tok

    def finish(self):
        final = {}
        for (sem, val, _) in self.out_tokens:
            k = id(sem)
            if k not in final or final[k][1] < val:
                final[k] = (sem, val)
        fw = list(final.values())
        for e in ("pe", "act", "dve", "pool"):
            if self.cnt[e]:
                fw.append((self.sem[e], self.cnt[e]))

        def emit(e, fw=fw):
            for (s, v) in fw:
                e.wait_ge(s, v)
        self.q["sp"].append(emit)
        nc = self.nc
        q = self.q
        with nc.allow_non_contiguous_dma(reason="small strided param / state loads"), nc.Block() as block:
            @block.tensor
            def _(e):
                for f in q["pe"]:
                    f(e)

            @block.scalar
            def _(e):
                for f in q["act"]:
                    f(e)

            @block.vector
            def _(e):
                for f in q["dve"]:
                    f(e)

            @block.gpsimd
            def _(e):
                for f in q["pool"]:
                    f(e)

            @block.sync
            def _(e):
                for f in q["sp"]:
                    f(e)


CO = {}
_off = 0
for _n, _w in [("ident", 128), ("mc128", 128), ("mbd64", 128), ("mbd8", 128), ("rm2", 2), ("rm16", 16),
               ("sl128", 128), ("sl64", 128), ("sl8", 128), ("rs64", 512), ("rs128", 512), ("rs8", 128), ("id3", 32)]:
    CO[_n] = (_off, _w)
    _off += _w
NCONST = _off


def make_consts():
    c = np.zeros((128, NCONST), np.float32)
    s = np.arange(128)[:, None]
    t = np.arange(128)[None, :]

    def put(n, a):
        o, w = CO[n]
        c[:, o:o + w] = a
    put("ident", (s == t))
    put("mc128", (s <= t))
    put("mbd64", (s <= t) & (s // 64 == t // 64))
    put("mbd8", (s <= t) & (s // 8 == t // 8))
    put("rm2", (s // 64 == np.arange(2)[None, :]))
    put("rm16", (s // 8 == np.arange(16)[None, :]))
    put("sl128", (s == 127) & (t >= 0))
    put("sl64", (s == 64 * (t // 64) + 63))
    put("sl8", (s == 8 * (t // 8) + 7))
    tt = np.arange(512)[None, :]
    put("rs64", np.broadcast_to((tt % 64 != 0), (128, 512)))
    put("rs128", np.broadcast_to((tt % 128 != 0), (128, 512)))
    put("rs8", np.broadcast_to((np.arange(128)[None, :] % 8 != 0), (128, 128)))
    put("id3", (s % 32 == np.arange(32)[None, :]) & (s < 96))
    return c


WNAMES = [("g_mix", [4, 1024]), ("g_ffn", [4, 1024]), ("g_ple", [4, 1024]), ("g_final", [1024]),
          ("w_even_in", [2, 1024, 3072]), ("hgrn_lb", [2, 512]), ("hgrn_gnorm", [2, 512]),
          ("lru_conv_w", [2, 4, 512]), ("lru_conv_b", [2, 512]), ("lru_wa", [2, 4, 128, 128]),
          ("lru_ba", [2, 4, 128]), ("lru_wx", [2, 4, 128, 128]), ("lru_bx", [2, 4, 128]), ("lru_lam", [2, 512]),
          ("w_even_out", [2, 1024, 1024]), ("ssm_in", [2, 1024, 5152]), ("ssm_conv_w", [2, 4, 3072]),
          ("ssm_conv_b", [2, 3072]), ("ssm_dt_bias", [2, 32]), ("ssm_a_log", [2, 32]), ("ssm_d", [2, 32]),
          ("ssm_gnorm", [2, 2048]), ("ssm_out", [2, 2048, 1024]), ("ffn_w1", [4, 1024, 2816]),
          ("ffn_w3", [4, 1024, 2816]), ("ffn_w2", [4, 2816, 1024]), ("ple_up", [4, 256, 1024]),
          ("ple_gate", [4, 1024, 1024])]

IN_SPECS = [("xp", [SEQ, D]), ("xs", [128, D]), ("pp", [4, SEQ, PLE]), ("psm", [4, 128, PLE]),
            ("st_hgrn", [2, 16, 4, 128, 128]), ("st_lru_h", [2, 16, 512]), ("st_lru_conv", [2, 48, 512]),
            ("st_ssm", [2, 16, 2048, 128]), ("st_ssm_conv", [2, 48, 3072]), ("consts", [128, NCONST])] + WNAMES

OUT_SPECS = [("yp", [SEQ, D]), ("ys", [128, D]), ("hg_p", [2, 4, 128, 128]), ("hg_s", [2, 16, 4, 128, 128]),
             ("lh_p", [2, 1, 512]), ("lh_s", [2, 16, 512]), ("lc_p", [2, 3, 512]), ("lc_s", [2, 48, 512]),
             ("ss_p", [2, 2048, 128]), ("ss_s", [2, 16, 2048, 128]), ("sc_p", [2, 3, 3072]), ("sc_s", [2, 48, 3072])]

NQ = 202


def _fs(ap):
    n = 1
    for d in ap.shape[1:]:
        n *= d
    return n


class V:
    def __init__(self, ap, keys):
        self.ap = ap
        self.keys = keys

    @property
    def bf(self):
        return self.ap.bitcast(BF16)


class Builder:
    def __init__(self, cfg):
        self.cfg = cfg
        self.nc = bass.Bass("TRN2", target_bir_lowering=False)
        self.st = ExitStack()

    def act(self, out, in_, func, r, w, scale=1.0, bias=None):
        kw = {}
        if bias is not None:
            kw["bias"] = bias
        self.S.op("act", lambda e: e.activation(out=out, in_=in_, func=func, scale=scale, **kw), r, w, cost=_fs(out))

    def tt(self, out, a, b, op, r, w, eng="dve", fence=False):
        self.S.op(eng, lambda e: e.tensor_tensor(out=out, in0=a, in1=b, op=op), r, w, fence=fence, cost=_fs(out))

    def ts(self, out, a, s1, s2, op0, op1, r, w, eng="dve"):
        if s2 is None:
            self.S.op(eng, lambda e: e.tensor_scalar(out=out, in0=a, scalar1=s1, scalar2=None, op0=op0), r, w, cost=_fs(out))
        else:
            self.S.op(eng, lambda e: e.tensor_scalar(out=out, in0=a, scalar1=s1, scalar2=s2, op0=op0, op1=op1), r, w, cost=_fs(out))

    def stt(self, out, a, s, b, op0, op1, r, w, eng="dve"):
        self.S.op(eng, lambda e: e.scalar_tensor_tensor(out=out, in0=a, scalar=s, in1=b, op0=op0, op1=op1), r, w, cost=_fs(out))

    def cp(self, out, in_, r, w, eng="dve", fence=False):
        if eng == "act":
            self.S.op("act", lambda e: e.copy(out, in_), r, w, fence=fence, cost=_fs(out))
        else:
            self.S.op(eng, lambda e: e.tensor_copy(out=out, in_=in_), r, w, fence=fence, cost=_fs(out))

    def mm(self, out, lhsT, rhs, start, stop, r, w):
        self.S.op("pe", lambda e: e.matmul(out, lhsT, rhs, start=start, stop=stop), r, w)

    def tr(self, out, in_, r, w):
        np_ = in_.shape[0]
        idn = self.ident[0:np_, 0:np_]
        self.S.op("pe", lambda e: e.transpose(out, in_, idn), list(r) + ["consts"], w)

    def ps(self, hold=False):
        i = self.ps_i
        while i in self.ps_hold:
            i = (i + 1) % 8
        self.ps_i = (i + 1) % 8
        if hold:
            self.ps_hold.add(i)
        return self.psb[i], ("ps", i)

    def qreset(self):
        self.q_i = 0

    def q(self, cols):
        n = (cols + 127) // 128
        assert self.q_i + n <= NQ, ("scratch overflow", self.q_i, n)
        i0 = self.q_i
        self.q_i += n
        self.q_max = max(getattr(self, "q_max", 0), self.q_i)
        return V(self.scr[:, i0 * 128:i0 * 128 + cols], [("q", i) for i in range(i0, i0 + n)])

    def dump(self, name, ap, keys):
        if self.dry or not self.cfg.get("dbg"):
            return
        shp = list(ap.shape)
        d = self.nc.dram_tensor("dbg_" + name, shp, ap.dtype, kind="ExternalOutput").ap()
        self.S.dma("sp", [(d, ap)], reads=keys, semres=None, is_output=True)

    def cst(self, name, rows=128):
        o, w = CO[name]
        return self.consts[0:rows, o:o + w]

    def _slot(self, sid):
        if sid < self.NSLOT:
            return self.wslot[sid], [("w", sid)]
        b0 = NQ - 16 * (sid - self.NSLOT + 1)
        return self.scr[:, b0 * 128:(b0 + 16) * 128].bitcast(BF16), [("q", b) for b in range(b0, b0 + 16)]

    def wreq(self, src, kch, ncols, pairs_fn=None):
        if self.dry:
            self.wlist.append((src, kch, ncols, pairs_fn, self.cur_ring))
            return None, None
        if self.w_i == 0:
            self.wslot_of = []
            self.wprev = []
            last = {}
            cnt = {}
            for n, g in enumerate(self.wlist):
                ring = g[4]
                c = cnt.get(ring, 0)
                cnt[ring] = c + 1
                sid = c % ring
                self.wslot_of.append(sid)
                self.wprev.append(last.get(sid, -1))
                last[sid] = n
        i = self.w_i
        self.w_i += 1
        while self.w_issued < len(self.wlist) and self.wprev[self.w_issued] < i and self.w_issued < i + 6:
            n = self.w_issued
            s_, k_, c_, pf, _ = self.wlist[n]
            st, keys = self._slot(self.wslot_of[n])
            view = st[:, 0:k_ * c_].rearrange("p (k c) -> p k c", k=k_)
            if pf is not None:
                pairs = pf(view)
            else:
                pairs = [(view, s_.rearrange("(k p) c -> p k c", p=128))]
            self.S.dma("pool", pairs, writes=keys, semres=("w", self.wslot_of[n]))
            self.w_issued += 1
        assert self.w_issued > i
        s_, k_, c_, pf, _ = self.wlist[i]
        st, keys = self._slot(self.wslot_of[i])
        return st[:, 0:k_ * c_].rearrange("p (k c) -> p k c", k=k_), keys

    def dense(self, src, kch, ncols, rhs, rkeys, n, outfn, csub=128):
        wv, wk = self.wreq(src, kch, ncols)
        if self.dry:
            return
        pieces = [(0, min(n, 512))] + ([(512, n - 512)] if n > 512 else [])
        for ci in range(ncols // csub):
            banks = [self.ps() for _ in pieces]
            for k in range(kch):
                r = rhs(k)
                for (bank, bk), (p0, w) in zip(banks, pieces):
                    self.mm(bank[0:csub, 0:w], wv[:, k, ci * csub:(ci + 1) * csub], r[:, p0:p0 + w], k == 0, k == kch - 1,
                            wk + list(rkeys(k)), [bk])
            for (bank, bk), (p0, w) in zip(banks, pieces):
                if n > 512:
                    outfn(ci, bank, bk, p0, w)
                else:
                    outfn(ci, bank, bk)

    def rms(self, src, skeys, nk, n, dim, outfn):
        pieces = [(0, min(n, 512))] + ([(512, n - 512)] if n > 512 else [])
        banks = [self.ps() for _ in pieces]
        for k in range(nk):
            sq = self.sqb[self.sq_i % 4]
            sk = ("sqb", self.sq_i % 4)
            self.sq_i += 1
            self.act(sq[:, 0:n], src(k), AF.Square, list(skeys(k)), [sk])
            for (bank, bk), (p0, w) in zip(banks, pieces):
                self.mm(bank[:, 0:w], self.ones_bf[:, :], sq[:, p0:p0 + w], k == 0, k == nk - 1, [sk, "consts2"], [bk])
        rs = self.rstd[0]
        rk = ("rstd", 0)
        for (bank, bk), (p0, w) in zip(banks, pieces):
            self.act(rs[:, p0:p0 + w], bank[:, 0:w], AF.Ln, [bk, "consts2"], [rk], scale=1.0 / dim, bias=self.epsc[:, 0:1])
        self.act(rs[:, 0:n], rs[:, 0:n], AF.Exp, [rk], [rk], scale=-0.5)
        for k in range(nk):
            outfn(k, rs[:, 0:n], rk)

    def xnorm(self, gcol, n, c0=0):
        def o(k, rs, rk):
            self.stt(self.xn[:, k, c0:c0 + n], self.xres[:, k, c0:c0 + n], gcol(k), rs, OP.mult, OP.mult,
                     [("x", k), rk, "params"], [("xn", k)])
        self.rms(lambda k: self.xres[:, k, c0:c0 + n], lambda k: [("x", k)], KD, n, D, o)

    def tok_to_fm(self, dst, dkeys, src_dram, rows, nchunk):
        mark = self.q_i
        stg = self.q(nchunk * 128)
        self.S.dma("sp", [(stg.ap[0:rows, :], src_dram)], writes=stg.keys, semres=stg.keys[0])
        for c0 in range(0, nchunk, 4):
            bank, bk = self.ps()
            nn = min(4, nchunk - c0)
            for i in range(nn):
                self.tr(bank[:, i * rows:(i + 1) * rows], stg.ap[0:rows, (c0 + i) * 128:(c0 + i + 1) * 128], stg.keys, [bk])
            for i in range(nn):
                self.cp(dst(c0 + i), bank[:, i * rows:(i + 1) * rows], [bk], dkeys(c0 + i), eng="act")
        self.q_i = mark

    def fm_to_tok(self, dst_dram, src, skeys, rows, nchunk, semres):
        mark = self.q_i
        stg = self.q(nchunk * 128)
        for c0 in range(0, nchunk, 4):
            bank, bk = self.ps()
            nn = min(4, nchunk - c0)
            for i in range(nn):
                self.tr(bank[0:rows, i * 128:(i + 1) * 128], src(c0 + i), skeys(c0 + i), [bk])
            self.cp(stg.ap[0:rows, c0 * 128:(c0 + nn) * 128], bank[0:rows, 0:nn * 128], [bk], stg.keys, eng="act")
        self.S.dma("sp", [(dst_dram, stg.ap[0:rows, :])], reads=stg.keys, semres=None, is_output=True)
        self.q_i = mark

    def build(self):
        nc, st = self.nc, self.st
        cfg = self.cfg
        self.S = S = Sch(nc, st)
        dr = {}
        for n, shp in IN_SPECS:
            dr[n] = nc.dram_tensor(n, shp, F32, kind="ExternalInput").ap()
        for n, shp in OUT_SPECS:
            dr[n] = nc.dram_tensor(n, shp, F32, kind="ExternalOutput").ap()
        self.dr = dr
        sb = lambda name, shape, dt: st.enter_context(nc.sbuf_tensor(name, shape, dt))
        self.psb = [st.enter_context(nc.psum_tensor("ps%d" % i, [128, 512], F32)) for i in range(8)]
        self.ps_i = 0
        self.consts = sb("consts_sb", [128, NCONST], F32)
        self.ident = self.cst("ident")
        self.ones_bf = sb("ones_bf", [128, 128], BF16)
        self.id3_bf = sb("id3_bf", [96, 32], BF16)
        self.epsc = sb("epsc", [128, 2], F32)
        self.xres = sb("xres", [128, KD, NW], F32)
        self.xn = sb("xn", [128, KD, NW], BF16)
        self.sqb = [sb("sqb%d" % i, [128, NW], BF16) for i in range(4)]
        self.sq_i = 0
        self.rstd = [sb("rstd%d" % i, [128, NW], F32) for i in range(1)]
        self.rs_i = 0
        self.NSLOT = 3
        self.wslot = [sb("wslot%d" % i, [128, 4096], BF16) for i in range(self.NSLOT)]
        self.scr = sb("scr", [128, NQ * 128], F32)
        P = {}
        P["g_mix"] = sb("p_gmix", [128, 4, 8], F32)
        P["g_ffn"] = sb("p_gffn", [128, 4, 8], F32)
        P["g_ple"] = sb("p_gple", [128, 4, 8], F32)
        P["g_final"] = sb("p_gfin", [128, 8], F32)
        P["lb"] = sb("p_lb", [128, 2, 4], F32)
        P["oml"] = sb("p_oml", [128, 2, 4], F32)
        P["hgn"] = sb("p_hgn", [128, 2, 4], F32)
        P["lcw"] = sb("p_lcw", [128, 2, 4, 4], F32)
        P["lcb"] = sb("p_lcb", [128, 2, 4], F32)
        P["lwa"] = sb("p_lwa", [128, 2, 4, 128], BF16)
        P["lwx"] = sb("p_lwx", [128, 2, 4, 128], BF16)
        P["lba"] = sb("p_lba", [128, 2, 4], F32)
        P["lbx"] = sb("p_lbx", [128, 2, 4], F32)
        P["lam"] = sb("p_lam", [128, 2, 4], F32)
        P["sp8"] = sb("p_sp8", [128, 2, 4], F32)
        P["sp16"] = sb("p_sp16", [128, 2, 4], F32)
        P["scw"] = sb("p_scw", [128, 2, 4, 24], F32)
        P["scb"] = sb("p_scb", [128, 2, 24], F32)
        P["dtb"] = sb("p_dtb", [96, 2], F32)
        P["A3"] = sb("p_A3", [96, 2], F32)
        P["Dp"] = sb("p_Dp", [128, 2, 16], F32)
        P["sgn"] = sb("p_sgn", [128, 2, 16], F32)
        self.P = P
        self.hgS = [sb("hgS%d" % j, [128, 4, 128], F32) for j in range(2)]
        self.hgSbf = sb("hgSbf", [128, 3, 4, 128], BF16)
        self.lruh = [sb("lruh%d" % j, [128, 4], F32) for j in range(2)]
        self.lruc = [sb("lruc%d" % j, [128, 4, 3], F32) for j in range(2)]
        self.ssmST = [sb("ssmST%d" % j, [128, 2048], F32) for j in range(2)]
        self.ssmSTbf = sb("ssmSTbf", [128, 2048], BF16)
        self.ssmc = [sb("ssmc%d" % j, [128, 24, 3], F32) for j in range(2)]

        for dry in (True, False):
            self.dry = dry
            if dry:
                self.wlist = []
                self.real_S = self.S
                self.S = _NullSch()
            else:
                self.S = self.real_S
                self.w_i = 0
                self.w_issued = 0
            self.ps_i = 0
            self.ps_hold = set()
            self.sq_i = 0
            self.rs_i = 0
            self.program()
        self.S.finish()
        return nc

    def program(self):
        S, dr, P = self.S, self.dr, self.P
        cfg = self.cfg
        S.dma("sp", [(self.consts[:], dr["consts"])], writes=["consts"], semres="consts")
        S.op("dve", lambda e: e.memset(self.ones_bf[:], 1.0), [], ["consts2"])
        S.op("dve", lambda e: e.memset(self.epsc[:, 0:1], EPS), [], ["consts2"])
        S.op("dve", lambda e: e.memset(self.epsc[:, 1:2], 1.0), [], ["consts2"])
        self.cp(self.id3_bf[:], self.cst("id3", 96), ["consts"], ["consts2"])
        self.qreset()
        if not self.dry:
            for nm in ("g_mix", "g_ffn", "g_ple"):
                self.tok_to_fm(lambda c, nm=nm: P[nm][:, :, c], lambda c: ["params"], dr[nm], 4, 8)
            self.tok_to_fm(lambda c: P["g_final"][:, c:c + 1], lambda c: ["params"], dr["g_final"].rearrange("(o d) -> o d", o=1), 1, 8)
            self.tok_to_fm(lambda c: P["lb"][:, :, c], lambda c: ["params"], dr["hgrn_lb"], 2, 4)
            self.tok_to_fm(lambda c: P["hgn"][:, :, c], lambda c: ["params"], dr["hgrn_gnorm"], 2, 4)
            self.tok_to_fm(lambda c: P["lam"][:, :, c], lambda c: ["params"], dr["lru_lam"], 2, 4)
        prs = []
        for r3 in range(3):
            prs.append((P["A3"][r3 * 32:(r3 + 1) * 32, :], dr["ssm_a_log"].rearrange("j h -> h j")))
        S.dma("sp", prs, writes=["paramsA3"], semres="params")
        prs = []
        for j in range(2):
            prs.append((P["lcw"][:, j], dr["lru_conv_w"][j].rearrange("k (c p) -> p k c", p=128)))
        prs.append((P["lcb"][:], dr["lru_conv_b"].rearrange("j (c p) -> p j c", p=128)))
        prs.append((P["lba"][:], dr["lru_ba"].rearrange("j h o -> o j h")))
        prs.append((P["lbx"][:], dr["lru_bx"].rearrange("j h o -> o j h")))
        for j in range(2):
            prs.append((P["scw"][:, j], dr["ssm_conv_w"][j].rearrange("k (c p) -> p k c", p=128)))
        prs.append((P["scb"][:], dr["ssm_conv_b"].rearrange("j (c p) -> p j c", p=128)))
        prs.append((P["sgn"][:], dr["ssm_gnorm"].rearrange("j (c p) -> p j c", p=128)))
        for r3 in range(3):
            prs.append((P["dtb"][r3 * 32:(r3 + 1) * 32, :], dr["ssm_dt_bias"].rearrange("j h -> h j")))
        for j in range(2):
            for hh in range(2):
                srcd = dr["ssm_d"][j].rearrange("(q t) -> t q", t=2)[hh:hh + 1, :]
                prs.append((P["Dp"][hh * 64:(hh + 1) * 64, j, :], srcd.broadcast_to([64, 16])))
        S.dma("act", prs, writes=["paramsB"], semres="paramsB")
        S.dma("pool", [(P["lwa"][:], dr["lru_wa"].rearrange("j h i o -> i j h o")),
                       (P["lwx"][:], dr["lru_wx"].rearrange("j h i o -> i j h o"))], writes=["params_bf"], semres="params_bf")
        tq = self.epsc
        self.qreset()
        tmp = self.q(64)
        t = tmp.ap
        self.act(t[:, 0:8], P["lb"][:].rearrange("p j h -> p (j h)"), AF.Exp, ["params"], tmp.keys)
        self.tt(t[:, 8:12], t[:, 0:4], t[:, 4:8], OP.add, tmp.keys, tmp.keys)
        S.op("dve", lambda e: e.reciprocal(t[:, 8:12], t[:, 8:12]), tmp.keys, tmp.keys)
        S.op("dve", lambda e: e.memset(P["oml"][:, 0, :], 1.0), [], ["params2"])
        self.tt(P["oml"][:, 1, :], t[:, 0:4], t[:, 8:12], OP.mult, tmp.keys, ["params2"])
        self.act(t[:, 16:24], P["lam"][:].rearrange("p j h -> p (j h)"), AF.Exp, ["params"], tmp.keys, scale=-1.0)
        self.act(t[:, 24:32], t[:, 16:24], AF.Ln, tmp.keys, tmp.keys, bias=self.epsc[:, 1:2])
        self.ts(P["sp8"][:].rearrange("p j h -> p (j h)"), t[:, 24:32], -8.0, None, OP.mult, None, tmp.keys, ["params2"])
        self.ts(P["sp16"][:].rearrange("p j h -> p (j h)"), t[:, 24:32], -16.0, None, OP.mult, None, tmp.keys, ["params2"])
        self.act(P["A3"][:], P["A3"][:], AF.Exp, ["paramsA3"], ["params2"])
        self.ts(P["A3"][:], P["A3"][:], -1.0, None, OP.mult, None, ["params2"], ["params2"])

        tiles = cfg.get("tiles", [0, 1, 2, 3])
        for tl in tiles:
            self.run_tile(tl)

    def run_tile(self, tl):
        S, dr, P = self.S, self.dr, self.P
        cfg = self.cfg
        parts = [(tl, 0, 512)]
        if tl == 0 and cfg.get("with_s", True):
            parts = [("S", 512, 128)] + parts
        if tl == "S":
            parts = [("S", 0, 128)]
        ntot = sum(p[2] for p in parts)
        self.cur_ring = 3
        self.qreset()
        for (pt, c0, n) in parts:
            xsrc = dr["xs"] if pt == "S" else dr["xp"][pt * 512:(pt + 1) * 512, :]
            for b in range(n // 128):
                stg = self.q(1024)
                S.dma("sp", [(stg.ap[:, :], xsrc[b * 128:(b + 1) * 128, :])], writes=stg.keys, semres=stg.keys[0])
                for k0 in range(0, KD, 4):
                    bank, bk = self.ps()
                    for i in range(4):
                        self.tr(bank[:, i * 128:(i + 1) * 128], stg.ap[:, (k0 + i) * 128:(k0 + i + 1) * 128], stg.keys, [bk])
                    self.cp(self.xres[:, k0:k0 + 4, c0 + b * 128:c0 + (b + 1) * 128], bank[:].rearrange("p (k t) -> p k t", k=4), [bk],
                            [("x", k0 + i) for i in range(4)], eng="act")
        for i in range(cfg.get("depth", DEPTH)):
            j = i // 2
            for (pt, c0, n) in parts:
                self.qreset()
                if i % 2 == 0:
                    if cfg.get("even", True):
                        self.even_mixer(pt, i, j, n, c0)
                else:
                    if cfg.get("odd", True):
                        self.odd_mixer(pt, i, j, n, c0)
            self.qreset()
            hold = {}
            if cfg.get("ple", True):
                hold["pT"] = [self.q(ntot // 2) for _ in range(2)]
            if cfg.get("ffn", True):
                self.ffn(i, ntot, pre=(lambda: self.ple_pre(parts, i, ntot, hold["pT"])) if "pT" in hold else None)
            elif "pT" in hold:
                self.ple_pre(parts, i, ntot, hold["pT"])
            if cfg.get("ple", True):
                self.ple(parts, i, ntot, hold["pT"])
        self.qreset()
        n = ntot
        yf = [self.q(n) for k in range(KD)]

        def o(k, rs, rk):
            self.stt(yf[k].ap, self.xres[:, k, 0:n], P["g_final"][:, k:k + 1], rs, OP.mult, OP.mult,
                     [("x", k), rk, "params"], yf[k].keys)
        self.rms(lambda k: self.xres[:, k, 0:n], lambda k: [("x", k)], KD, n, D, o)
        for (pt, c0, np_) in parts:
            ydst = dr["ys"] if pt == "S" else dr["yp"][pt * 512:(pt + 1) * 512, :]
            stgs = [self.q(1024) for _ in range(2)]
            for b in range(np_ // 128):
                stg = stgs[b % 2]
                for k0 in range(0, KD, 4):
                    bank, bk = self.ps()
                    for i in range(4):
                        self.tr(bank[:, i * 128:(i + 1) * 128], yf[k0 + i].ap[:, c0 + b * 128:c0 + (b + 1) * 128], yf[k0 + i].keys, [bk])
                    self.cp(stg.ap[:, k0 * 128:(k0 + 4) * 128], bank[:], [bk], stg.keys, eng="act")
                S.dma("sp", [(ydst[b * 128:(b + 1) * 128, :], stg.ap[:, :])], reads=stg.keys, semres=stg.keys[0], is_output=True)

    def ffn(self, i, n, pre=None):
        S, dr, P = self.S, self.dr, self.P
        self.xnorm(lambda k: P["g_ffn"][:, i, k:k + 1], n)
        if pre is not None:
            pre()
        hT = [self.q(n // 2) for f in range(KFF)]
        sil = [self.q(n) for _ in range(2)]
        si = [0]
        w1, w3, w2 = dr["ffn_w1"][i], dr["ffn_w3"][i], dr["ffn_w2"][i]
        rhs = lambda k: self.xn[:, k, 0:n]
        rk = lambda k: [("xn", k)]
        for c0 in range(0, DFF, 256):
            held = {}

            def o1(ci, bank, bk, p0=0, w=n, held=held):
                if p0 == 0:
                    held[ci] = sil[si[0] % 2]
                    si[0] += 1
                sv = held[ci]
                self.act(sv.ap[:, p0:p0 + w], bank[:, 0:w], AF.Silu, [bk], sv.keys)

            def o3(ci, bank, bk, p0=0, w=n, c0=c0, held=held):
                f = c0 // 128 + ci
                self.tt(hT[f].bf[:, p0:p0 + w], held[ci].ap[:, p0:p0 + w], bank[:, 0:w], OP.mult, [bk] + held[ci].keys, hT[f].keys)
            self.dense(w1[:, c0:c0 + 256], KD, 256, rhs, rk, n, o1)
            self.dense(w3[:, c0:c0 + 256], KD, 256, rhs, rk, n, o3)
        for c0 in range(0, D, 128):
            def o2(ci, bank, bk, p0=0, w=n, c0=c0):
                dc = c0 // 128 + ci
                self.tt(self.xres[:, dc, p0:p0 + w], self.xres[:, dc, p0:p0 + w], bank[:, 0:w], OP.add, [bk, ("x", dc)], [("x", dc)])
            self.dense(w2[:, c0:c0 + 128], KFF, 128, lambda k: hT[k].bf[:, 0:n], lambda k: hT[k].keys, n, o2)

    def ple_pre(self, parts, i, n, pT):
        S, dr, P = self.S, self.dr, self.P
        if not self.dry:
            stgs = [self.q(256) for _ in range(2)]
            bi = 0
            for (pt, c0, np_) in parts:
                psrc = dr["psm"][i] if pt == "S" else dr["pp"][i, pt * 512:(pt + 1) * 512, :]
                for b in range(np_ // 128):
                    stg = stgs[bi % 2]
                    bi += 1
                    S.dma("sp", [(stg.ap, psrc[b * 128:(b + 1) * 128, :])], writes=stg.keys, semres=stg.keys[0])
                    bank, bk = self.ps()
                    for c in range(2):
                        self.tr(bank[:, c * 128:(c + 1) * 128], stg.ap[:, c * 128:(c + 1) * 128], stg.keys, [bk])
                    for c in range(2):
                        self.cp(pT[c].bf[:, c0 + b * 128:c0 + (b + 1) * 128], bank[:, c * 128:(c + 1) * 128], [bk], pT[c].keys, eng="act")

    def ple(self, parts, i, n, pT):
        S, dr, P = self.S, self.dr, self.P
        if not self.dry:
            for k in range(KD):
                self.cp(self.xn[:, k, 0:n], self.xres[:, k, 0:n], [("x", k)], [("xn", k)], eng="act")
        gate = [self.q(n) for _ in range(KD)]
        for c0 in range(0, D, 512):
            def og(ci, bank, bk, p0=0, w=n, c0=c0):
                dc = c0 // 128 + ci
                self.act(gate[dc].ap[:, p0:p0 + w], bank[:, 0:w], AF.Sigmoid, [bk], gate[dc].keys)
            self.dense(dr["ple_gate"][i][:, c0:c0 + 512], KD, 512, lambda k: self.xn[:, k, 0:n], lambda k: [("xn", k)], n, og)
        for c0 in range(0, D, 512):
            def oe(ci, bank, bk, p0=0, w=n, c0=c0):
                dc = c0 // 128 + ci
                self.tt(gate[dc].ap[:, p0:p0 + w], gate[dc].ap[:, p0:p0 + w], bank[:, 0:w], OP.mult, [bk] + gate[dc].keys, gate[dc].keys)
            self.dense(dr["ple_up"][i][:, c0:c0 + 512], 2, 512, lambda k: pT[k].bf[:, 0:n], lambda k: pT[k].keys, n, oe)
        if self.dry:
            return
        tmp = [self.q(n) for _ in range(2)]

        def o(k, rs, rk):
            tv = tmp[k % 2]
            self.stt(tv.ap, gate[k].ap, P["g_ple"][:, i, k:k + 1], rs, OP.mult, OP.mult, gate[k].keys + [rk, "params"], tv.keys)
            self.tt(self.xres[:, k, 0:n], self.xres[:, k, 0:n], tv.ap, OP.add, tv.keys + [("x", k)], [("x", k)])
        self.rms(lambda k: gate[k].ap, lambda k: gate[k].keys, KD, n, D, o)

    def even_mixer(self, tl, i, j, n, c0x=0):
        S, dr, P = self.S, self.dr, self.P
        samp = (tl == "S")
        nblk = n // 128
        L = LS if samp else 64
        nsb = 128 // L
        nseg = n // L
        W = dr["w_even_in"][j]
        self.xnorm(lambda k: P["g_mix"][:, i, k:k + 1], n, c0x)
        rhs = lambda k: self.xn[:, k, c0x:c0x + n]
        rk = lambda k: [("xn", k)]
        A = [self.q(n) for _ in range(4)]
        B = [self.q(n) for _ in range(4)]
        C = [self.q(n) for _ in range(4)]
        Dq = [self.q(n) for _ in range(4)]
        E = [self.q(n) for _ in range(4)]
        KDf = [self.q(n) for _ in range(4)]
        QT = [self.q(n // 2) for _ in range(4)]
        KT = [self.q(n // 2) for _ in range(4)]
        VT = [self.q(256) for _ in range(nblk)]
        KDT = [self.q(256) for _ in range(nblk)]
        act8 = [self.q(n // 2) for _ in range(8)]
        rsname = "rs8" if samp else "rs64"
        rmask = self.cst(rsname)[:, 0:n]
        def of(hc, bank, bk):
            self.act(A[hc].ap, bank[:, 0:n], AF.Sigmoid, [bk], A[hc].keys, scale=-1.0)
        self.dense(W[:, 512:1024], KD, 512, rhs, rk, n, of)
        if not self.dry:
            H4 = range(4)
            for hc in H4:
                self.ts(A[hc].ap, A[hc].ap, P["oml"][:, j, hc:hc + 1], None, OP.mult, None, A[hc].keys + ["params2"], A[hc].keys)
            for hc in H4:
                self.act(B[hc].ap, A[hc].ap, AF.Ln, A[hc].keys + ["consts2"], B[hc].keys, scale=-1.0, bias=self.epsc[:, 1:2])
            for hc in H4:
                S.op("dve", lambda e, hc=hc: e.tensor_tensor_scan(out=C[hc].ap, data0=rmask, data1=B[hc].ap, initial=0.0,
                                                                  op0=OP.mult, op1=OP.add), B[hc].keys + ["consts"], C[hc].keys, cost=2 * n)
            for hc in H4:
                self.act(B[hc].ap, C[hc].ap, AF.Exp, C[hc].keys, B[hc].keys)
                self.act(Dq[hc].ap, C[hc].ap, AF.Exp, C[hc].keys, Dq[hc].keys, scale=-1.0)
            for hc in H4:
                self.tt(Dq[hc].ap, A[hc].ap, Dq[hc].ap, OP.mult, A[hc].keys + Dq[hc].keys, Dq[hc].keys)
            for hc in H4:
                self.cp(KT[hc].bf[:, 0:n], Dq[hc].ap, Dq[hc].keys, KT[hc].keys, eng="act")
            for hc in H4:
                ebl = B[hc].ap.rearrange("p (s l) -> p s l", l=L)[:, :, L - 1:L].broadcast_to([128, nseg, L])
                self.tt(KDf[hc].ap.rearrange("p (s l) -> p s l", l=L), Dq[hc].ap.rearrange("p (s l) -> p s l", l=L), ebl,
                        OP.mult, Dq[hc].keys + B[hc].keys, KDf[hc].keys)
        def oq(hc, bank, bk):
            self.act(E[hc].ap, bank[:, 0:n], AF.Silu, [bk], E[hc].keys)
        self.dense(W[:, 0:512], KD, 512, rhs, rk, n, oq)
        wv, wk = self.wreq(W[:, 1024:1536], KD, 512)
        if not self.dry:
            for b in range(nblk):
                bank, bk = self.ps()
                for k in range(KD):
                    self.mm(bank[:, :], self.xn[:, k, c0x + b * 128:c0x + (b + 1) * 128], wv[:, k, :], k == 0, k == KD - 1, wk + [("xn", k)], [bk])
                self.cp(VT[b].bf[:, 0:512], bank[:, :], [bk], VT[b].keys, eng="act")
        if not self.dry:
            for hc in range(4):
                self.tt(QT[hc].bf[:, 0:n], E[hc].ap, B[hc].ap, OP.mult, E[hc].keys + B[hc].keys, QT[hc].keys)
        if not self.dry:
            for b in range(nblk):
                bank, bk = self.ps()
                for hc in range(4):
                    self.tr(bank[:, hc * 128:(hc + 1) * 128], KDf[hc].ap[:, b * 128:(b + 1) * 128], KDf[hc].keys, [bk])
                self.cp(KDT[b].bf[:, 0:512], bank[:, :], [bk], KDT[b].keys, eng="act")
        YB = A
        def oy(cc, bank, bk):
            self.cp(YB[cc].ap, bank[:, 0:n], [bk], YB[cc].keys, eng="act")
        self.dense(W[:, 2048:2560], KD, 512, rhs, rk, n, oy)
        if samp:
            ext = [self.q(16 * 11) for _ in range(4)]
            extv = [x.ap.rearrange("p (s t) -> p s t", t=11) for x in ext]
        else:
            ext = [self.q(n + 3) for _ in range(4)]
        if samp and not self.dry:
            cin = self.q(4 * 48)
            cinv = cin.ap.rearrange("p (c r) -> p c r", c=4)
            self.tok_to_fm(lambda c: cinv[:, c, :], lambda c: cin.keys, dr["st_lru_conv"][j], 48, 4)
            h0 = self.q(4 * 16)
            h0v = h0.ap.rearrange("p (c r) -> p c r", c=4)
            self.tok_to_fm(lambda c: h0v[:, c, :], lambda c: h0.keys, dr["st_lru_h"][j], 16, 4)
        U = KDf
        def ou(cc, bank, bk):
            if samp:
                self.cp(extv[cc][:, :, 3:11], bank[:, 0:n].rearrange("p (s t) -> p s t", t=8), [bk], ext[cc].keys, eng="act")
                self.cp(extv[cc][:, :, 0:3], cinv[:, cc, :].rearrange("p (s r) -> p s r", r=3), cin.keys, ext[cc].keys)
                taps = [extv[cc][:, :, k:k + 8] for k in range(4)]
                uo = U[cc].ap.rearrange("p (s t) -> p s t", t=8)
            else:
                self.cp(ext[cc].ap[:, 3:3 + n], bank[:, 0:n], [bk], ext[cc].keys, eng="act")
                if tl == 0:
                    S.op("dve", lambda e: e.memset(ext[cc].ap[:, 0:3], 0.0), [], ext[cc].keys)
                else:
                    self.cp(ext[cc].ap[:, 0:3], self.lruc[j][:, cc, :], [("lruc", j, cc)], ext[cc].keys, eng="act")
                self.cp(self.lruc[j][:, cc, :], ext[cc].ap[:, n:n + 3], ext[cc].keys, [("lruc", j, cc)], eng="act")
                taps = [ext[cc].ap[:, k:k + n] for k in range(4)]
                uo = U[cc].ap
            self.act(U[cc].ap, bank[:, 0:n], AF.Identity, [bk, "paramsB"], U[cc].keys, scale=P["lcw"][:, j, 3, cc:cc + 1], bias=P["lcb"][:, j, cc:cc + 1])
            for k in range(0, 3):
                self.stt(uo, taps[k], P["lcw"][:, j, k, cc:cc + 1], uo, OP.mult, OP.add, ext[cc].keys + U[cc].keys + ["paramsB"], U[cc].keys)
        self.dense(W[:, 2560:3072], KD, 512, rhs, rk, n, ou)
        G = Dq
        def og(hc, bank, bk):
            self.act(G[hc].ap, bank[:, 0:n], AF.Silu, [bk], G[hc].keys)
        if self.dry:
            self.dense(W[:, 1536:2048], KD, 512, rhs, rk, n, og)
            wv, wk = self.wreq(dr["w_even_out"][j][:, 0:512], KD, 512)
            wv, wk = self.wreq(dr["w_even_out"][j][:, 512:1024], KD, 512)
            return
        Ub = [self.q(n // 2) for _ in range(4)]
        R = C
        GI = Dq
        AA = E
        MM = U
        C4 = range(4)

        def lru1():
            for cc in C4:
                self.cp(Ub[cc].bf[:, 0:n], U[cc].ap, U[cc].keys, Ub[cc].keys, eng="act")
            for cc in C4:
                bank, bk = self.ps()
                self.mm(bank[:, 0:n], P["lwa"][:, j, cc, :], Ub[cc].bf[:, 0:n], True, True, Ub[cc].keys + ["params_bf"], [bk])
                self.act(R[cc].ap, bank[:, 0:n], AF.Sigmoid, [bk, "paramsB"], R[cc].keys, bias=P["lba"][:, j, cc:cc + 1])
                bank, bk = self.ps()
                self.mm(bank[:, 0:n], P["lwx"][:, j, cc, :], Ub[cc].bf[:, 0:n], True, True, Ub[cc].keys + ["params_bf"], [bk])
                self.act(GI[cc].ap, bank[:, 0:n], AF.Sigmoid, [bk, "paramsB"], GI[cc].keys, bias=P["lbx"][:, j, cc:cc + 1])

        def lru2():
            for cc in C4:
                self.act(AA[cc].ap, R[cc].ap, AF.Exp, R[cc].keys + ["params2"], AA[cc].keys, scale=P["sp8"][:, j, cc:cc + 1])
            for cc in C4:
                self.tt(GI[cc].ap, GI[cc].ap, U[cc].ap, OP.mult, GI[cc].keys + U[cc].keys, GI[cc].keys)
            for cc in C4:
                self.act(MM[cc].ap, R[cc].ap, AF.Exp, R[cc].keys + ["params2"], MM[cc].keys, scale=P["sp16"][:, j, cc:cc + 1])
            for cc in C4:
                self.act(MM[cc].ap, MM[cc].ap, AF.Sqrt, MM[cc].keys + ["consts2"], MM[cc].keys, scale=-1.0, bias=self.epsc[:, 1:2])
            if (not samp) and tl == 0:
                for cc in C4:
                    S.op("dve", lambda e, cc=cc: e.memset(MM[cc].ap[:, 0:1], 1.0), MM[cc].keys, MM[cc].keys)
                    S.op("dve", lambda e, cc=cc: e.memset(self.lruh[j][:, cc:cc + 1], 0.0), [], [("lruh", j, cc)])
            for cc in C4:
                self.tt(GI[cc].ap, GI[cc].ap, MM[cc].ap, OP.mult, GI[cc].keys + MM[cc].keys, GI[cc].keys, fence=True)

        def lru3():
            for cc in C4:
                if samp:
                    for sq in range(16):
                        S.op("dve", lambda e, cc=cc, sq=sq: e.tensor_tensor_scan(
                            out=R[cc].ap[:, sq * 8:(sq + 1) * 8], data0=AA[cc].ap[:, sq * 8:(sq + 1) * 8], data1=GI[cc].ap[:, sq * 8:(sq + 1) * 8],
                            initial=h0v[:, cc, sq:sq + 1], op0=OP.mult, op1=OP.add), AA[cc].keys + GI[cc].keys + h0.keys, R[cc].keys, fence=(sq == 0))
                else:
                    S.op("dve", lambda e, cc=cc: e.tensor_tensor_scan(
                        out=R[cc].ap, data0=AA[cc].ap, data1=GI[cc].ap, initial=self.lruh[j][:, cc:cc + 1], op0=OP.mult, op1=OP.add),
                        AA[cc].keys + GI[cc].keys + [("lruh", j, cc)], R[cc].keys, fence=True, cost=2 * n)
            if not samp:
                for cc in C4:
                    self.cp(self.lruh[j][:, cc:cc + 1], R[cc].ap[:, n - 1:n], R[cc].keys, [("lruh", j, cc)], fence=True)

        def lru4():
            for cc in C4:
                self.act(AA[cc].ap, YB[cc].ap, AF.Square, YB[cc].keys, AA[cc].keys)
            for cc in C4:
                self.ts(AA[cc].ap, AA[cc].ap, 0.044715, 1.0, OP.mult, OP.add, AA[cc].keys, AA[cc].keys)
                self.tt(AA[cc].ap, AA[cc].ap, YB[cc].ap, OP.mult, AA[cc].keys + YB[cc].keys, AA[cc].keys)
            for cc in C4:
                self.act(AA[cc].ap, AA[cc].ap, AF.Sigmoid, AA[cc].keys, AA[cc].keys, scale=1.5957691216057308)
            for cc in C4:
                self.tt(AA[cc].ap, AA[cc].ap, YB[cc].ap, OP.mult, AA[cc].keys + YB[cc].keys, AA[cc].keys)
                self.tt(act8[4 + cc].bf[:, 0:n], AA[cc].ap, R[cc].ap, OP.mult, AA[cc].keys + R[cc].keys, act8[4 + cc].keys)

        lru_stages = [lru1, lru2, lru3, lru4]
        if True:
            if samp:
                S0 = self.q(64 * 128)
                S0v = S0.ap.rearrange("p (s h e) -> p s h e", s=16, h=4)
                S.dma("sp", [(S0v[:, sq], dr["st_hgrn"][j, sq].rearrange("h d e -> d h e")) for sq in range(16)],
                      writes=S0.keys, semres=S0.keys[0])
                S0bf = self.q(32 * 128)
                S0bv = S0bf.bf.rearrange("p (s h e) -> p s h e", s=16, h=4)
                self.cp(S0bf.bf, S0.ap, S0.keys, S0bf.keys, eng="act")
                Snew = S0
                Snv = S0v
                rmk = self.cst("rm16")
                mname = "mbd8"
            else:
                hgS = self.hgS[j]
                if tl == 0:
                    S.op("dve", lambda e: e.memset(hgS[:], 0.0), [], [("hgS", j)])
                rmk = self.cst("rm2")
                mname = "mbd64"
            mask = self.cst(mname)
            SCq = [self.q(256) for _ in range(2)]
            VTm = [self.q(256) for _ in range(2)]
            F = [self.q(n) for _ in range(4)]
            vi = 0
            for b in range(nblk):
                if not samp:
                    self.cp(self.hgSbf[:, 0], hgS[:], [("hgS", j)], [("hgSbf", 0)], eng="act")
                bank, bk = self.ps()
                for hc in range(4):
                    self.mm(bank[:, hc * 128:(hc + 1) * 128], KT[hc].bf[:, b * 128:(b + 1) * 128], QT[hc].bf[:, b * 128:(b + 1) * 128],
                            True, True, KT[hc].keys + QT[hc].keys, [bk])
                sc = SCq[b % 2]
                self.tt(sc.bf[:, 0:512].rearrange("p (h t) -> p h t", h=4), bank[:, :].rearrange("p (h t) -> p h t", h=4),
                        mask.unsqueeze(1).broadcast_to([128, 4, 128]), OP.mult, [bk, "consts"], sc.keys)
                if nsb == 2:
                    vms = []
                    for sg in range(nsb):
                        vm = VTm[sg]
                        self.ts(vm.bf[:, 0:512], VT[b].bf[:, 0:512], rmk[:, sg:sg + 1], None, OP.mult, None, VT[b].keys + ["consts"], vm.keys)
                        vms.append(vm)
                for sg in range(nsb):
                    gseg = b * nsb + sg
                    if nsb == 2:
                        vm = vms[sg]
                    else:
                        vm = VTm[vi % 2]
                        vi += 1
                        self.ts(vm.bf[:, 0:512], VT[b].bf[:, 0:512], rmk[:, sg:sg + 1], None, OP.mult, None, VT[b].keys + ["consts"], vm.keys)
                    bank, bk = self.ps()
                    for hc in range(4):
                        self.mm(bank[:, hc * 128:(hc + 1) * 128], KDT[b].bf[:, hc * 128:(hc + 1) * 128], vm.bf[:, hc * 128:(hc + 1) * 128],
                                True, True, KDT[b].keys + vm.keys, [bk])
                    for hc in range(4):
                        ebl = B[hc].ap[:, gseg * L + L - 1:gseg * L + L]
                        if samp:
                            self.stt(Snv[:, sg, hc, :], S0v[:, sg, hc, :], ebl, bank[:, hc * 128:(hc + 1) * 128], OP.mult, OP.add,
                                     S0.keys + B[hc].keys + [bk], Snew.keys)
                        else:
                            self.stt(hgS[:, hc, :], hgS[:, hc, :], ebl, bank[:, hc * 128:(hc + 1) * 128], OP.mult, OP.add,
                                     [("hgS", j), bk] + B[hc].keys, [("hgS", j)])
                    if not samp:
                        self.cp(self.hgSbf[:, sg + 1], hgS[:], [("hgS", j)], [("hgSbf", sg + 1)], eng="act")
                bank, bk = self.ps()
                for hc in range(4):
                    self.mm(bank[:, hc * 128:(hc + 1) * 128], VT[b].bf[:, hc * 128:(hc + 1) * 128], sc.bf[:, hc * 128:(hc + 1) * 128],
                            True, False, VT[b].keys + sc.keys, [bk])
                    for sg in range(nsb):
                        c0 = b * 128 + sg * L
                        if samp:
                            lh = S0bv[:, sg, hc, :]
                            lk = S0bf.keys
                        else:
                            lh = self.hgSbf[:, sg, hc, :]
                            lk = [("hgSbf", sg)]
                        self.mm(bank[:, hc * 128 + sg * L:hc * 128 + (sg + 1) * L], lh, QT[hc].bf[:, c0:c0 + L],
                                False, sg == nsb - 1, lk + QT[hc].keys, [bk])
                for hc in range(4):
                    self.cp(F[hc].ap[:, b * 128:(b + 1) * 128], bank[:, hc * 128:(hc + 1) * 128], [bk], F[hc].keys, eng="act")
                if b < len(lru_stages):
                    lru_stages[b]()
            for st_ in lru_stages[nblk:]:
                st_()
            if samp:
                S.dma("sp", [(dr["hg_s"][j, sq].rearrange("h d e -> d h e"), Snv[:, sq]) for sq in range(16)],
                      reads=Snew.keys, semres=Snew.keys[0], is_output=True)
            elif tl == self.cfg.get("last", 3):
                S.dma("sp", [(dr["hg_p"][j].rearrange("h d e -> d h e"), hgS[:])], reads=[("hgS", j)], semres=("hgS", j), is_output=True)
        self.dense(W[:, 1536:2048], KD, 512, rhs, rk, n, og)
        for hc in range(4):
            def o(k, rs, rkk, hc=hc):
                self.stt(F[hc].ap, F[hc].ap, P["hgn"][:, j, hc:hc + 1], rs, OP.mult, OP.mult, F[hc].keys + [rkk, "params"], F[hc].keys)
                self.tt(act8[hc].bf[:, 0:n], F[hc].ap, G[hc].ap, OP.mult, F[hc].keys + G[hc].keys, act8[hc].keys)
            self.rms(lambda k, hc=hc: F[hc].ap, lambda k, hc=hc: F[hc].keys, 1, n, 128, o)
        if samp:
            hl = self.q(64)
            hlv = hl.ap.rearrange("p (c s) -> p c s", c=4)
            for cc in range(4):
                self.cp(hlv[:, cc, :], R[cc].ap[:, 7:128:8], R[cc].keys, hl.keys, fence=True)
            self.fm_to_tok(dr["lh_s"][j], lambda c: hlv[:, c, :], lambda c: hl.keys, 16, 4, ("lh_s", j))
            cl = self.q(4 * 48)
            clv = cl.ap.rearrange("p (c s r) -> p c s r", c=4, r=3)
            for cc in range(4):
                self.cp(clv[:, cc], extv[cc][:, :, 8:11], ext[cc].keys, cl.keys)
            clf = cl.ap.rearrange("p (c x) -> p c x", c=4)
            self.fm_to_tok(dr["lc_s"][j], lambda c: clf[:, c, :], lambda c: cl.keys, 48, 4, ("lc_s", j))
        elif tl == self.cfg.get("last", 3):
            self.fm_to_tok(dr["lh_p"][j], lambda c: self.lruh[j][:, c:c + 1], lambda c: [("lruh", j, c)], 1, 4, ("lh_p", j))
            self.fm_to_tok(dr["lc_p"][j], lambda c: self.lruc[j][:, c, :], lambda c: [("lruc", j, c)], 3, 4, ("lc_p", j))
        for c0 in range(0, D, 512):
            def oo(ci, bank, bk, c0=c0):
                dc = c0 // 128 + ci
                self.tt(self.xres[:, dc, c0x:c0x + n], self.xres[:, dc, c0x:c0x + n], bank[:, 0:n], OP.add, [bk, ("x", dc)], [("x", dc)])
            self.dense(dr["w_even_out"][j][:, c0:c0 + 512], KD, 512, lambda k: act8[k].bf[:, 0:n], lambda k: act8[k].keys, n, oo)

    def odd_mixer(self, tl, i, j, n, c0x=0):
        S, dr, P = self.S, self.dr, self.P
        samp = (tl == "S")
        nblk = n // 128
        W = dr["ssm_in"][j]
        self.xnorm(lambda k: P["g_mix"][:, i, k:k + 1], n, c0x)
        rhs = lambda k: self.xn[:, k, c0x:c0x + n]
        rk = lambda k: [("xn", k)]
        XY = [self.q(n) for _ in range(16)]
        small = self.q(4 * n)
        sm = small.ap
        c3q = self.q(n // 2)
        c3 = c3q.bf[:, 0:n]
        tk = self.q(5 * nblk * 32)
        tkv = tk.ap[:, 0:5 * nblk * 32].rearrange("p (a b h) -> p a b h", a=5, b=nblk)
        negcum, cumtok, dttok, dte, elb = [tkv[:, a] for a in range(5)]
        def pf(view):
            return [(view[:, :, r3 * 32:(r3 + 1) * 32], W[:, 5120:5152].rearrange("(k p) c -> p k c", p=128)) for r3 in range(3)]
        wv, wk = self.wreq(W[:, 5120:5152], KD, 96, pf)
        if not self.dry:
            bank, bk = self.ps()
            for k in range(KD):
                self.mm(bank[0:96, 0:n], wv[:, k, :], self.xn[:, k, c0x:c0x + n], k == 0, k == KD - 1, wk + [("xn", k)], [bk])
            xs_, dt_, cum_, r1_ = sm[0:96, 0:n], sm[0:96, n:2 * n], sm[0:96, 2 * n:3 * n], sm[0:96, 3 * n:4 * n]
            sk = small.keys
            self.ts(xs_, bank[0:96, 0:n], P["dtb"][:, j:j + 1], None, OP.add, None, [bk, "paramsB"], sk)
            self.act(r1_, xs_, AF.Abs, sk, sk)
            self.act(r1_, r1_, AF.Exp, sk, sk, scale=-1.0)
            self.act(r1_, r1_, AF.Ln, sk + ["consts2"], sk, bias=self.epsc[0:96, 1:2])
            self.stt(dt_, xs_, 0.0, r1_, OP.max, OP.add, sk, sk)
            self.ts(xs_, dt_, P["A3"][:, j:j + 1], None, OP.mult, None, sk + ["params2"], sk)
            rmask = self.cst("rs8" if samp else "rs128", 96)[:, 0:n]
            S.op("dve", lambda e: e.tensor_tensor_scan(out=cum_, data0=rmask, data1=xs_, initial=0.0, op0=OP.mult, op1=OP.add),
                 sk + ["consts"], sk)
            self.cp(c3[0:96, :], cum_, sk, c3q.keys)
            self.tt(r1_[0:96, :], cum_[0:96, :], c3[0:96, :], OP.subtract, sk + c3q.keys, sk)
            self.cp(c3[32:64, :], r1_[32:64, :], sk, c3q.keys)
            self.cp(c3[64:96, :], r1_[64:96, :], sk, c3q.keys)
            self.tt(r1_[64:96, :], r1_[64:96, :], c3[64:96, :], OP.subtract, sk + c3q.keys, sk)
            self.cp(c3[64:96, :], r1_[64:96, :], sk, c3q.keys)
        BT = [self.q(n // 2) for _ in range(4)]
        CT = [self.q(n // 2) for _ in range(4)]
        BF = [self.q(n) for _ in range(4)]
        if samp and not self.dry:
            cout = self.q(24 * 48)
            coutv = cout.ap.rearrange("p (c s r) -> p c s r", c=24, r=3)
        mark_pre = self.q_i
        if samp:
            ext = [self.q(16 * 11) for _ in range(4)]
            extv = [x.ap.rearrange("p (s t) -> p s t", t=11) for x in ext]
            if not self.dry:
                cin = self.q(24 * 48)
                cinv = cin.ap.rearrange("p (c r) -> p c r", c=24)
                self.tok_to_fm(lambda c: cinv[:, c, :], lambda c: cin.keys, dr["st_ssm_conv"][j], 48, 24)
        else:
            ext = [self.q(n + 3) for _ in range(4)]
        NACC = 4
        acc = [self.q(n) for _ in range(NACC)]
        ai = [0]
        pend = []
        for gi in range(6):
            def oc(ci, bank, bk, gi=gi):
                c = gi * 4 + ci
                av = acc[ai[0] % NACC]
                ai[0] += 1
                if samp:
                    self.cp(extv[ci][:, :, 3:11], bank[:, 0:n].rearrange("p (s t) -> p s t", t=8), [bk], ext[ci].keys, eng="act")
                    self.cp(extv[ci][:, :, 0:3], cinv[:, c, :].rearrange("p (s r) -> p s r", r=3), cin.keys, ext[ci].keys)
                    self.cp(coutv[:, c], extv[ci][:, :, 8:11], ext[ci].keys, cout.keys)
                    taps = [extv[ci][:, :, k:k + 8] for k in range(4)]
                    uo = av.ap.rearrange("p (s t) -> p s t", t=8)
                else:
                    self.cp(ext[ci].ap[:, 3:3 + n], bank[:, 0:n], [bk], ext[ci].keys, eng="act")
                    if tl == 0:
                        S.op("dve", lambda e: e.memset(ext[ci].ap[:, 0:3], 0.0), [], ext[ci].keys)
                    else:
                        self.cp(ext[ci].ap[:, 0:3], self.ssmc[j][:, c, :], [("ssmc", j, c)], ext[ci].keys, eng="act")
                    self.cp(self.ssmc[j][:, c, :], ext[ci].ap[:, n:n + 3], ext[ci].keys, [("ssmc", j, c)], eng="act")
                    taps = [ext[ci].ap[:, k:k + n] for k in range(4)]
                    uo = av.ap
                self.act(av.ap, bank[:, 0:n], AF.Identity, [bk, "paramsB"], av.keys, scale=P["scw"][:, j, 3, c:c + 1], bias=P["scb"][:, j, c:c + 1])
                while pend:
                    pend.pop(0)()
                for k in range(0, 3):
                    self.stt(uo, taps[k], P["scw"][:, j, k, c:c + 1], uo, OP.mult, OP.add, ext[ci].keys + av.keys + ["paramsB"], av.keys)

                def fin(gi=gi, ci=ci, c=c, av=av):
                    if gi < 4:
                        self.act(XY[c].ap, av.ap, AF.Silu, av.keys, XY[c].keys)
                    elif gi == 4:
                        self.act(BF[ci].ap, av.ap, AF.Silu, av.keys, BF[ci].keys)
                        self.cp(BT[ci].bf[:, 0:n], BF[ci].ap, BF[ci].keys, BT[ci].keys)
                    else:
                        self.act(CT[ci].bf[:, 0:n], av.ap, AF.Silu, av.keys, CT[ci].keys)
                pend.append(fin)
            self.dense(W[:, 2048 + gi * 512:2048 + (gi + 1) * 512], KD, 512, rhs, rk, n, oc)
        while pend:
            pend.pop(0)()
        if not self.dry:
            slname = "sl8" if samp else "sl128"
            for b in range(nblk):
                bank, bk = self.ps()
                self.tr(bank[:, 0:32], cum_[0:32, b * 128:(b + 1) * 128], sk, [bk])
                self.tr(bank[:, 32:64], dt_[0:32, b * 128:(b + 1) * 128], sk, [bk])
                self.cp(cumtok[:, b, :], bank[:, 0:32], [bk], tk.keys, eng="act")
                self.cp(dttok[:, b, :], bank[:, 32:64], [bk], tk.keys, eng="act")
                self.ts(negcum[:, b, :], bank[:, 0:32], -1.0, None, OP.mult, None, [bk], tk.keys)
                bank2, bk2 = self.ps()
                self.mm(bank2[:, 0:32], self.cst(slname), cumtok[:, b, :], True, True, tk.keys + ["consts"], [bk2])
                self.act(elb[:, b, :], bank2[:, 0:32], AF.Exp, [bk2], tk.keys)
                self.tt(dte[:, b, :], bank2[:, 0:32], negcum[:, b, :], OP.add, [bk2] + tk.keys, tk.keys)
                self.act(dte[:, b, :], dte[:, b, :], AF.Exp, tk.keys, tk.keys)
        if not self.dry and j == 0:
            self.dump("xy0", XY[0].ap, XY[0].keys)
            self.dump("xy4", XY[4].ap, XY[4].keys)
            self.dump("xy8", XY[8].ap, XY[8].keys)
        self.q_i = mark_pre
        if not self.dry:
            self.ssd_scan(tl, j, n, XY, BT, CT, BF, c3, c3q, small, tk, negcum, cumtok, dttok, dte, elb,
                          (coutv, cout) if samp else None)
            self.q_i = mark_pre
        ZS = [self.q(n) for _ in range(2)]
        zi = [0]
        YN = [self.q(n // 2) for _ in range(16)]
        pendz = []
        for gi in range(4):
            def oz(ci, bank, bk, gi=gi):
                c = gi * 4 + ci
                zv = ZS[zi[0] % 2]
                zi[0] += 1
                self.act(zv.ap, bank[:, 0:n], AF.Silu, [bk], zv.keys)
                self.tt(XY[c].ap, XY[c].ap, zv.ap, OP.mult, XY[c].keys + zv.keys, XY[c].keys)
            self.dense(W[:, gi * 512:(gi + 1) * 512], KD, 512, rhs, rk, n, oz)
            if not self.dry:
                while pendz:
                    pendz.pop(0)()

                def gn(gi=gi):
                    def o(k, rs, rkk, gi=gi):
                        c = gi * 4 + k
                        self.stt(YN[c].bf[:, 0:n], XY[c].ap, P["sgn"][:, j, c:c + 1], rs, OP.mult, OP.mult, XY[c].keys + [rkk, "paramsB"], YN[c].keys)
                    self.rms(lambda k, gi=gi: XY[gi * 4 + k].ap, lambda k, gi=gi: XY[gi * 4 + k].keys, 4, n, 512, o)
                pendz.append(gn)
        while pendz:
            pendz.pop(0)()
        for c0 in range(0, D, 256):
            def oo(ci, bank, bk, c0=c0):
                dc = c0 // 128 + ci
                self.tt(self.xres[:, dc, c0x:c0x + n], self.xres[:, dc, c0x:c0x + n], bank[:, 0:n], OP.add, [bk, ("x", dc)], [("x", dc)])
            self.dense(dr["ssm_out"][j][:, c0:c0 + 256], 16, 256, lambda k: YN[k].bf[:, 0:n], lambda k: YN[k].keys, n, oo)

    def ssd_scan(self, tl, j, n, XY, BT, CT, BF, c3, c3q, small, tk, negcum, cumtok, dttok, dte, elb, coutp):
        S, dr, P = self.S, self.dr, self.P
        samp = (tl == "S")
        nblk = n // 128
        ST = self.ssmST[j]
        STbf = self.ssmSTbf
        if samp:
            coutv, cout = coutp
            cof = cout.ap.rearrange("p (c x) -> p c x", c=24)
            self.fm_to_tok(dr["sc_s"][j], lambda c: cof[:, c, :], lambda c: cout.keys, 48, 24, ("sc_s", j))
        else:
            if tl == 0:
                S.op("dve", lambda e: e.memset(ST[:], 0.0), [], [("ST", j, g) for g in range(4)])
            self.cp(STbf[:], ST[:], [("ST", j, g) for g in range(4)], [("STbf", g) for g in range(4)], eng="act")
        mask = self.cst("mbd8" if samp else "mc128")
        mark_tmp = self.q_i
        NB2 = 1 if samp else 2
        CBm_ = [self.q(256) for _ in range(NB2)]
        XDT_ = [self.q(1024) for _ in range(NB2)]
        XW_ = [self.q(1024) for _ in range(NB2)]
        BTOK_ = [self.q(256) for _ in range(NB2)]
        NR = 3
        L4 = [self.q(256) for _ in range(NR)]
        SC4 = [self.q(256) for _ in range(NR)]
        CT4 = [self.q(256) for _ in range(8 if samp else NR)]
        E4 = [self.q(512) for _ in range(NR)]
        tmpS_ = [self.q(512) for _ in range(2)]
        NC3 = self.q(n // 2)
        nc3 = NC3.bf[:, 0:n]
        if not samp or True:
            self.ts(nc3[0:96, :], c3[0:96, :], -1.0, None, OP.mult, None, c3q.keys, NC3.keys)

        def pre(b):
            bs = slice(b * 128, (b + 1) * 128)
            CBm, XDT, XW, BTOK = CBm_[b % NB2], XDT_[b % NB2], XW_[b % NB2], BTOK_[b % NB2]
            xdt3 = XDT.bf[:, 0:2048].rearrange("p (h d) -> p h d", h=32)
            xw3 = XW.bf[:, 0:2048].rearrange("p (h d) -> p h d", h=32)
            bank, bk = self.ps()
            for g in range(4):
                self.mm(bank[:, g * 128:(g + 1) * 128], BT[g].bf[:, bs], CT[g].bf[:, bs], True, True, BT[g].keys + CT[g].keys, [bk])
            self.tt(CBm.bf[:, 0:512].rearrange("p (g t) -> p g t", g=4), bank[:, :].rearrange("p (g t) -> p g t", g=4),
                    mask.unsqueeze(1).broadcast_to([128, 4, 128]), OP.mult, [bk, "consts"], CBm.keys)
            for q4 in range(4):
                bank, bk = self.ps()
                for i4 in range(4):
                    self.tr(bank[:, i4 * 128:(i4 + 1) * 128], XY[q4 * 4 + i4].ap[:, bs], XY[q4 * 4 + i4].keys, [bk])
                self.tt(xdt3[:, q4 * 8:(q4 + 1) * 8, :], bank[:, :].rearrange("p (h d) -> p h d", h=8),
                        dttok[:, b, q4 * 8:(q4 + 1) * 8].unsqueeze(2).broadcast_to([128, 8, 64]), OP.mult, [bk] + tk.keys, XDT.keys)
            self.tt(xw3, xdt3, dte[:, b, :].unsqueeze(2).broadcast_to([128, 32, 64]), OP.mult, XDT.keys + tk.keys, XW.keys, eng=PEN)
            bank, bk = self.ps()
            for g in range(4):
                self.tr(bank[:, g * 128:(g + 1) * 128], BF[g].ap[:, bs], BF[g].keys, [bk])
            self.cp(BTOK.bf[:, 0:512], bank[:, :], [bk], BTOK.keys, eng="act")

        pre(0)
        for b in range(nblk):
            bs = slice(b * 128, (b + 1) * 128)
            CBm, XDT, XW, BTOK = CBm_[b % NB2], XDT_[b % NB2], XW_[b % NB2], BTOK_[b % NB2]
            xdt3 = XDT.bf[:, 0:2048].rearrange("p (h d) -> p h d", h=32)
            if b + 1 < nblk:
                pre(b + 1)
            cur = {}

            def stageA(hg, b=b, bs=bs, CBm=CBm):
                g = hg // 2
                bankc, bkc = self.ps()
                bankd, bkd = self.ps()
                for i4 in range(4):
                    h = hg * 4 + i4
                    sel = self.id3_bf[:, h:h + 1].broadcast_to([96, 128])
                    self.mm(bankc[:, i4 * 128:(i4 + 1) * 128], sel, c3[0:96, bs], True, True, c3q.keys + ["consts2"], [bkc])
                    self.mm(bankd[:, i4 * 128:(i4 + 1) * 128], sel, c3[0:96, bs], True, False, c3q.keys + ["consts2"], [bkd])
                    self.mm(bankd[:, i4 * 128:(i4 + 1) * 128], nc3[0:96, bs], sel, False, True, NC3.keys + ["consts2"], [bkd])
                l4, s4, e4 = L4[hg % NR], SC4[hg % NR], E4[hg % NR]
                c4 = CT4[hg % len(CT4)]
                self.act(l4.bf[:, 0:512], bankd[:, :], AF.Exp, [bkd], l4.keys)
                self.act(e4.ap, bankc[:, :], AF.Exp, [bkc], e4.keys)
                self.stt(s4.bf[:, 0:512].rearrange("p (h t) -> p h t", h=4), l4.bf[:, 0:512].rearrange("p (h t) -> p h t", h=4), 1.0,
                         CBm.bf[:, g * 128:(g + 1) * 128].unsqueeze(1).broadcast_to([128, 4, 128]), OP.min, OP.mult,
                         l4.keys + CBm.keys, s4.keys)
                self.tt(c4.bf[:, 0:512].rearrange("p (h t) -> p h t", h=4), e4.ap.rearrange("p (h t) -> p h t", h=4),
                        CT[g].bf[:, bs].unsqueeze(1).broadcast_to([128, 4, 128]), OP.mult, e4.keys + CT[g].keys, c4.keys, eng=PEN)

            def stageB(hg, b=b, bs=bs, XDT=XDT, xdt3=xdt3, cur=cur):
                s4 = SC4[hg % NR]
                c4 = CT4[hg % len(CT4)]
                if hg % 2 == 0:
                    cur["y"] = self.ps()
                banky, bky = cur["y"]
                for i4 in range(4):
                    h = hg * 4 + i4
                    po = (h % 2) * 64
                    col = ((h // 2) % 4) * 128
                    self.mm(banky[po:po + 64, col:col + 128], xdt3[:, h, :], s4.bf[:, i4 * 128:(i4 + 1) * 128], True, samp,
                            XDT.keys + s4.keys, [bky])
                    if not samp:
                        self.mm(banky[po:po + 64, col:col + 128], STbf[:, h * 64:(h + 1) * 64], c4.bf[:, i4 * 128:(i4 + 1) * 128], False, True,
                                [("STbf", h // 8)] + c4.keys, [bky])
                if hg % 2 == 1:
                    q4 = hg // 2
                    for i4 in range(4):
                        c = q4 * 4 + i4
                        self.stt(XY[c].ap[:, bs], XY[c].ap[:, bs], P["Dp"][:, j, c:c + 1], banky[:, i4 * 128:(i4 + 1) * 128], OP.mult, OP.add,
                                 XY[c].keys + [bky, "paramsB"], XY[c].keys)

            LA = 2
            for hg in range(8 + LA):
                if hg < 8:
                    stageA(hg)
                if hg >= LA:
                    stageB(hg - LA)
            if not samp:
                for g in range(4):
                    tmpS = tmpS_[g % 2]
                    bank, bk = self.ps()
                    self.mm(bank[:, :], BTOK.bf[:, g * 128:(g + 1) * 128], XW.bf[:, g * 512:(g + 1) * 512], True, True, BTOK.keys + XW.keys, [bk])
                    stg = ST[:, g * 512:(g + 1) * 512]
                    self.tt(tmpS.ap.rearrange("p (h d) -> p h d", h=8), stg.rearrange("p (h d) -> p h d", h=8),
                            elb[:, b, g * 8:(g + 1) * 8].unsqueeze(2).broadcast_to([128, 8, 64]), OP.mult, [("ST", j, g)] + tk.keys, tmpS.keys, eng=PEN)
                    self.tt(stg, tmpS.ap, bank[:, :], OP.add, tmpS.keys + [bk], [("ST", j, g)])
                    self.cp(STbf[:, g * 512:(g + 1) * 512], stg, [("ST", j, g)], [("STbf", g)], eng="act")
        XW, BTOK = XW_[0], BTOK_[0]
        if samp:
            if j == 0:
                self.dump("tk", tk.ap, tk.keys)
            self.ssd_sample_states(j, XY, CT4, XW, BTOK, small, c3q)
        elif tl == self.cfg.get("last", 3):
            self.q_i = mark_tmp
            stg = self.q(512)
            for q4 in range(4):
                bank, bk = self.ps()
                for i4 in range(4):
                    self.tr(bank[:, i4 * 128:(i4 + 1) * 128], ST[:, (q4 * 4 + i4) * 128:(q4 * 4 + i4 + 1) * 128], [("ST", j, q4)], [bk])
                self.cp(stg.ap, bank[:, :], [bk], stg.keys, eng="act")
                S.dma("sp", [(dr["ss_p"][j, q4 * 512:(q4 + 1) * 512, :].rearrange("(c p) n -> p c n", p=128),
                              stg.ap.rearrange("p (c n) -> p c n", c=4))], reads=stg.keys, semres=stg.keys[0], is_output=True)
            for hf in range(2):
                self.fm_to_tok(dr["sc_p"][j][:, hf * 1536:(hf + 1) * 1536], lambda c, hf=hf: self.ssmc[j][:, hf * 12 + c, :],
                               lambda c, hf=hf: [("ssmc", j, hf * 12 + c)], 3, 12, None)

    def ssd_sample_states(self, j, XY, CT4, XW, BTOK, small, c3q):
        S, dr, P = self.S, self.dr, self.P
        sm = small.ap
        cum_ = sm[0:96, 256:384]
        decq = self.q(256)
        decS = decq.ap.rearrange("p (c s) -> p c s", c=16)
        bank, bk = self.ps()
        for jc in range(16):
            for hh in range(2):
                self.mm(bank[hh * 64:(hh + 1) * 64, jc * 16:(jc + 1) * 16],
                        self.ident[0:32, 2 * jc + hh:2 * jc + hh + 1].broadcast_to([32, 64]),
                        cum_[0:32, 7:128:8], True, True, small.keys + ["consts"], [bk])
        self.act(decq.ap, bank[:, 0:256], AF.Exp, [bk], decq.keys)
        if j == 0:
            self.dump("decq", decq.ap, decq.keys)
            self.dump("small", small.ap, small.keys)
        ib = [self.ps(hold=True) for _ in range(4)]
        S0 = [self.q(2048) for _ in range(2)]
        S0T = [self.q(1024) for _ in range(2)]
        Sn = [self.q(2048) for _ in range(2)]
        Bm = [self.q(256) for _ in range(2)]
        rm = self.cst("rm16")
        xw3 = XW.bf[:, 0:2048]
        def load(sq):
            s0 = S0[sq % 2]
            s0v = s0.ap.rearrange("p (c n) -> p c n", c=16)
            S.dma("sp", [(s0v[:, hf * 8:(hf + 1) * 8, :], dr["st_ssm"][j, sq, hf * 1024:(hf + 1) * 1024, :].rearrange("(c p) n -> p c n", p=128))
                         for hf in range(2)], writes=s0.keys, semres=s0.keys[0])
        load(0)
        for sq in range(16):
            s0, s0t, sn, bm = S0[sq % 2], S0T[sq % 2], Sn[sq % 2], Bm[sq % 2]
            s0v = s0.ap.rearrange("p (c n) -> p c n", c=16)
            if sq + 1 < 16:
                load(sq + 1)
            for q4 in range(4):
                bank, bk = self.ps()
                for i4 in range(4):
                    self.tr(bank[:, i4 * 128:(i4 + 1) * 128], s0v[:, q4 * 4 + i4, :], s0.keys, [bk])
                self.cp(s0t.bf[:, q4 * 512:(q4 + 1) * 512], bank[:, :], [bk], s0t.keys, eng="act")
            for h in range(32):
                hg, i4 = h // 4, h % 4
                po = (h % 2) * 64
                col = ((h // 2) % 4) * 128 + sq * 8
                bnk, bkk = ib[h // 8]
                self.mm(bnk[po:po + 64, col:col + 8], s0t.bf[:, h * 64:(h + 1) * 64], CT4[hg].bf[:, i4 * 128 + sq * 8:i4 * 128 + sq * 8 + 8],
                        True, True, s0t.keys + CT4[hg].keys, [bkk])
            self.ts(bm.bf[:, 0:512], BTOK.bf[:, 0:512], rm[:, sq:sq + 1], None, OP.mult, None, BTOK.keys + ["consts"], bm.keys)
            snv = sn.ap.rearrange("p (c n) -> p c n", c=16)
            for q4 in range(4):
                bank, bk = self.ps()
                for i4 in range(4):
                    jc = q4 * 4 + i4
                    self.mm(bank[:, i4 * 128:(i4 + 1) * 128], xw3[:, jc * 128:(jc + 1) * 128], bm.bf[:, q4 * 128:(q4 + 1) * 128], True, True,
                            XW.keys + bm.keys, [bk])
                for i4 in range(4):
                    jc = q4 * 4 + i4
                    self.stt(snv[:, jc, :], s0v[:, jc, :], decS[:, jc, sq:sq + 1], bank[:, i4 * 128:(i4 + 1) * 128], OP.mult, OP.add,
                             s0.keys + decq.keys + [bk], sn.keys)
            S.dma("sp", [(dr["ss_s"][j, sq].rearrange("(c p) n -> p c n", p=128), snv)], reads=sn.keys, semres=sn.keys[0], is_output=True)
            if j == 0 and sq in (3, 8):
                self.dump("s0_%d" % sq, s0.ap, s0.keys)
                self.dump("sn_%d" % sq, sn.ap, sn.keys)
                self.dump("bm_%d" % sq, bm.bf[:, 0:512], bm.keys)
                self.dump("s0t_%d" % sq, s0t.bf[:, 0:2048], s0t.keys)
                if sq == 3:
                    self.dump("xw", XW.bf[:, 0:2048], XW.keys)
                    self.dump("btok", BTOK.bf[:, 0:512], BTOK.keys)
        for q4 in range(4):
            bnk, bkk = ib[q4]
            for i4 in range(4):
                c = q4 * 4 + i4
                self.tt(XY[c].ap[:, 0:128], XY[c].ap[:, 0:128], bnk[:, i4 * 128:(i4 + 1) * 128], OP.add, XY[c].keys + [bkk], XY[c].keys)
        self.ps_hold.clear()


class _NullSch:
    cnt = {e: 0 for e in ENGS}

    def op(self, *a, **k):
        return None

    def dma(self, *a, **k):
        return None


_CACHE = {}


def build_nc(cfg=None):
    cfg = cfg or {}
    b = Builder(cfg)
    nc = b.build()
    return nc, b


def shard_inputs(inp):
    consts = make_consts()
    maps = []
    for c in range(NCORES):
        sl = slice(16 * c, 16 * c + 16)
        m = {
            "xp": np.ascontiguousarray(inp["x_prompt"][c]),
            "xs": np.ascontiguousarray(inp["x_sample"][sl].reshape(128, D)),
            "pp": np.ascontiguousarray(inp["p_prompt"][:, c]),
            "psm": np.ascontiguousarray(inp["p_sample"][:, sl].reshape(4, 128, PLE)),
            "st_hgrn": np.ascontiguousarray(inp["state_hgrn"][:, sl]),
            "st_lru_h": np.ascontiguousarray(inp["state_lru_h"][:, sl]),
            "st_lru_conv": np.ascontiguousarray(inp["state_lru_conv"][:, sl].reshape(2, 48, 512)),
            "st_ssm": np.ascontiguousarray(inp["state_ssm"][:, sl].reshape(2, 16, 2048, 128)),
            "st_ssm_conv": np.ascontiguousarray(inp["state_ssm_conv"][:, sl].reshape(2, 48, 3072)),
            "consts": consts,
        }
        for n, _ in WNAMES:
            m[n] = np.ascontiguousarray(inp[n])
        maps.append(m)
    return maps


def gather_outputs(results):
    R = results
    cat = lambda n: [r[n] for r in R]
    y_prompt = np.stack(cat("yp"), 0)
    y_sample = np.concatenate([r["ys"].reshape(16, 8, D) for r in R], 0)
    hg_p = np.stack(cat("hg_p"), 1)
    hg_s = np.concatenate(cat("hg_s"), 1)
    lh_p = np.stack([r["lh_p"].reshape(2, 512) for r in R], 1)
    lh_s = np.concatenate(cat("lh_s"), 1)
    lc_p = np.stack(cat("lc_p"), 1)
    lc_s = np.concatenate([r["lc_s"].reshape(2, 16, 3, 512) for r in R], 1)
    ss_p = np.stack([r["ss_p"].reshape(2, 32, 64, 128) for r in R], 1)
    ss_s = np.concatenate([r["ss_s"].reshape(2, 16, 32, 64, 128) for r in R], 1)
    sc_p = np.stack(cat("sc_p"), 1)
    sc_s = np.concatenate([r["sc_s"].reshape(2, 16, 3, 3072) for r in R], 1)
    outs = (y_prompt, y_sample, hg_p, hg_s, lh_p, lh_s, lc_p, lc_s, ss_p, ss_s, sc_p, sc_s)
    return tuple(np.ascontiguousarray(o, dtype=np.float32) for o in outs)


def kernel(**inputs):
    inp = {k: np.asarray(v) for k, v in inputs.items()}
    nc, _ = build_nc()
    maps = shard_inputs(inp)
    res = run_bass_kernel_spmd(nc, maps, core_ids=list(range(NCORES)))
    return gather_outputs(res.results)
```

```python
import numpy as np
from contextlib import ExitStack
import concourse.bass as bass
import concourse.mybir as mybir
from concourse.bass_utils import run_bass_kernel_spmd

F32 = mybir.dt.float32
BF16 = mybir.dt.bfloat16
AF = mybir.ActivationFunctionType
OP = mybir.AluOpType

ENGS = ("pe", "act", "dve", "pool", "sp")
SAME_ENGINE_SYNC = False
PEN = "pool"
SELF_WINDOW = 2000

NCORES = 8
D = 1024
KD = 8
SEQ = 2048
NSEQ_S = 16
LS = 8
DEPTH = 4
DFF = 2816
KFF = 22
PLE = 256
EPS = 1e-6
NW = 640


class Sch:
    def __init__(self, nc, stack):
        self.nc = nc
        self.stack = stack
        self.q = {e: [] for e in ENGS}
        self.cnt = {e: 0 for e in ENGS}
        self.sem = {e: stack.enter_context(nc.semaphore("c_" + e)) for e in ENGS}
        self.seen = {e: {} for e in ENGS}
        self.lastw = {}
        self.readers = {}
        self.dsem = {}
        self.pool_i = 0
        self.out_tokens = []
        self.cyc = {e: 0 for e in ENGS}
        self.stamp = {e: {} for e in ENGS}
        self.n_self = 0

    def _need(self, eng, tok, waits, fence=False):
        if tok is None:
            return
        sem, val, src = tok
        if src == eng and not (SAME_ENGINE_SYNC or fence):
            if eng == "pe" or (eng != "pool" and self.cyc[eng] - self.stamp[eng].get(val, -10 ** 9) >= SELF_WINDOW):
                return
            self.n_self += 1
        key = id(sem)
        if self.seen[eng].get(key, 0) >= val:
            return
        cur = waits.get(key)
        if cur is None or cur[1] < val:
            waits[key] = (sem, val)

    def _deps(self, eng, reads, writes, fence=False):
        waits = {}
        for r in reads:
            self._need(eng, self.lastw.get(r), waits, fence)
        for w in writes:
            self._need(eng, self.lastw.get(w), waits)
            for t in self.readers.get(w, ()):
                self._need(eng, t, waits)
        for key, (sem, val) in waits.items():
            self.seen[eng][key] = val
        return list(waits.values())

    def _commit(self, tok, reads, writes):
        for r in reads:
            self.readers.setdefault(r, []).append(tok)
        for w in writes:
            self.lastw[w] = tok
            self.readers[w] = []

    def op(self, eng, fn, reads=(), writes=(), fence=False, cost=64):
        waits = self._deps(eng, reads, writes, fence)
        self.cnt[eng] += 1
        val = self.cnt[eng]
        self.cyc[eng] += max(64, cost)
        self.stamp[eng][val] = self.cyc[eng]
        sem = self.sem[eng]

        def emit(e, waits=waits, fn=fn, sem=sem):
            for (s, v) in waits:
                e.wait_ge(s, v)
            fn(e).then_inc(sem, 1)

        self.q[eng].append(emit)
        tok = (sem, val, eng)
        self._commit(tok, reads, writes)
        return tok

    NPOOL = 24

    def _get_dsem(self, res):
        if res is None or (isinstance(res, tuple) and res[0] == "q"):
            res = ("pool", self.pool_i % self.NPOOL)
            self.pool_i += 1
        d = self.dsem.get(res)
        if d is None:
            sem = self.stack.enter_context(self.nc.semaphore("d%d" % len(self.dsem)))
            d = [sem, 0]
            self.dsem[res] = d
        return d

    def dma(self, eng, pairs, reads=(), writes=(), semres=None, is_output=False):
        waits = self._deps(eng, reads, writes)
        d = self._get_dsem(semres)
        if d[1] > 0 and self.seen[eng].get(id(d[0]), 0) < d[1]:
            self.seen[eng][id(d[0])] = d[1]
            waits = [w for w in waits if w[0] is not d[0]] + [(d[0], d[1])]
        d[1] += 16 * len(pairs)
        sem, val = d[0], d[1]

        def emit(e, waits=waits, pairs=pairs, sem=sem):
            for (s, v) in waits:
                e.wait_ge(s, v)
            for (o, i) in pairs:
                e.dma_start(out=o, in_=i).then_inc(sem, 16)

        self.q[eng].append(emit)
        tok = (sem, val, None)
        self._commit(tok, reads, writes)
        if is_output:
            self.out_tokens.append(tok)
        return tok

    def finish(self):
        final = {}
        for (sem, val, _) in self.out_tokens:
            k = id(sem)
            if k not in final or final[k][1] < val:
                final[k] = (sem, val)
        fw = list(final.values())
        for e in ("pe", "act", "dve", "pool"):
            if self.cnt[e]:
                fw.append((self.sem[e], self.cnt[e]))

        def emit(e, fw=fw):
            for (s, v) in fw:
                e.wait_ge(s, v)
        self.q["sp"].append(emit)
        nc = self.nc
        q = self.q
        with nc.allow_non_contiguous_dma(reason="small strided param / state loads"), nc.Block() as block:
            @block.tensor
            def _(e):
                for f in q["pe"]:
                    f(e)

            @block.scalar
            def _(e):
                for f in q["act"]:
                    f(e)

            @block.vector
            def _(e):
                for f in q["dve"]:
                    f(e)

            @block.gpsimd
            def _(e):
                for f in q["pool"]:
                    f(e)

            @block.sync
            def _(e):
                for f in q["sp"]:
                    f(e)


CO = {}
_off = 0
for _n, _w in [("ident", 128), ("mc128", 128), ("mbd64", 128), ("mbd8", 128), ("rm2", 2), ("rm16", 16),
               ("sl128", 128), ("sl64", 128), ("sl8", 128), ("rs64", 512), ("rs128", 512), ("rs8", 128), ("id3", 32)]:
    CO[_n] = (_off, _w)
    _off += _w
NCONST = _off


def make_consts():
    c = np.zeros((128, NCONST), np.float32)
    s = np.arange(128)[:, None]
    t = np.arange(128)[None, :]

    def put(n, a):
        o, w = CO[n]
        c[:, o:o + w] = a
    put("ident", (s == t))
    put("mc128", (s <= t))
    put("mbd64", (s <= t) & (s // 64 == t // 64))
    put("mbd8", (s <= t) & (s // 8 == t // 8))
    put("rm2", (s // 64 == np.arange(2)[None, :]))
    put("rm16", (s // 8 == np.arange(16)[None, :]))
    put("sl128", (s == 127) & (t >= 0))
    put("sl64", (s == 64 * (t // 64) + 63))
    put("sl8", (s == 8 * (t // 8) + 7))
    tt = np.arange(512)[None, :]
    put("rs64", np.broadcast_to((tt % 64 != 0), (128, 512)))
    put("rs128", np.broadcast_to((tt % 128 != 0), (128, 512)))
    put("rs8", np.broadcast_to((np.arange(128)[None, :] % 8 != 0), (128, 128)))
    put("id3", (s % 32 == np.arange(32)[None, :]) & (s < 96))
    return c


WNAMES = [("g_mix", [4, 1024]), ("g_ffn", [4, 1024]), ("g_ple", [4, 1024]), ("g_final", [1024]),
          ("w_even_in", [2, 1024, 3072]), ("hgrn_lb", [2, 512]), ("hgrn_gnorm", [2, 512]),
          ("lru_conv_w", [2, 4, 512]), ("lru_conv_b", [2, 512]), ("lru_wa", [2, 4, 128, 128]),
          ("lru_ba", [2, 4, 128]), ("lru_wx", [2, 4, 128, 128]), ("lru_bx", [2, 4, 128]), ("lru_lam", [2, 512]),
          ("w_even_out", [2, 1024, 1024]), ("ssm_in", [2, 1024, 5152]), ("ssm_conv_w", [2, 4, 3072]),
          ("ssm_conv_b", [2, 3072]), ("ssm_dt_bias", [2, 32]), ("ssm_a_log", [2, 32]), ("ssm_d", [2, 32]),
          ("ssm_gnorm", [2, 2048]), ("ssm_out", [2, 2048, 1024]), ("ffn_w1", [4, 1024, 2816]),
          ("ffn_w3", [4, 1024, 2816]), ("ffn_w2", [4, 2816, 1024]), ("ple_up", [4, 256, 1024]),
          ("ple_gate", [4, 1024, 1024])]

IN_SPECS = [("xp", [SEQ, D]), ("xs", [128, D]), ("pp", [4, SEQ, PLE]), ("psm", [4, 128, PLE]),
            ("st_hgrn", [2, 16, 4, 128, 128]), ("st_lru_h", [2, 16, 512]), ("st_lru_conv", [2, 48, 512]),
            ("st_ssm", [2, 16, 2048, 128]), ("st_ssm_conv", [2, 48, 3072]), ("consts", [128, NCONST])] + WNAMES

OUT_SPECS = [("yp", [SEQ, D]), ("ys", [128, D]), ("hg_p", [2, 4, 128, 128]), ("hg_s", [2, 16, 4, 128, 128]),
             ("lh_p", [2, 1, 512]), ("lh_s", [2, 16, 512]), ("lc_p", [2, 3, 512]), ("lc_s", [2, 48, 512]),
             ("ss_p", [2, 2048, 128]), ("ss_s", [2, 16, 2048, 128]), ("sc_p", [2, 3, 3072]), ("sc_s", [2, 48, 3072])]

NQ = 202


def _fs(ap):
    n = 1
    for d in ap.shape[1:]:
        n *= d
    return n


class V:
    def __init__(self, ap, keys):
        self.ap = ap
        self.keys = keys

    @property
    def bf(self):
        return self.ap.bitcast(BF16)


class Builder:
    def __init__(self, cfg):
        self.cfg = cfg
        self.nc = bass.Bass("TRN2", target_bir_lowering=False)
        self.st = ExitStack()

    def act(self, out, in_, func, r, w, scale=1.0, bias=None):
        kw = {}
        if bias is not None:
            kw["bias"] = bias
        self.S.op("act", lambda e: e.activation(out=out, in_=in_, func=func, scale=scale, **kw), r, w, cost=_fs(out))

    def tt(self, out, a, b, op, r, w, eng="dve", fence=False):
        self.S.op(eng, lambda e: e.tensor_tensor(out=out, in0=a, in1=b, op=op), r, w, fence=fence, cost=_fs(out))

    def ts(self, out, a, s1, s2, op0, op1, r, w, eng="dve"):
        if s2 is None:
            self.S.op(eng, lambda e: e.tensor_scalar(out=out, in0=a, scalar1=s1, scalar2=None, op0=op0), r, w, cost=_fs(out))
        else:
            self.S.op(eng, lambda e: e.tensor_scalar(out=out, in0=a, scalar1=s1, scalar2=s2, op0=op0, op1=op1), r, w, cost=_fs(out))

    def stt(self, out, a, s, b, op0, op1, r, w, eng="dve"):
        self.S.op(eng, lambda e: e.scalar_tensor_tensor(out=out, in0=a, scalar=s, in1=b, op0=op0, op1=op1), r, w, cost=_fs(out))

    def cp(self, out, in_, r, w, eng="dve", fence=False):
        if eng == "act":
            self.S.op("act", lambda e: e.copy(out, in_), r, w, fence=fence, cost=_fs(out))
        else:
            self.S.op(eng, lambda e: e.tensor_copy(out=out, in_=in_), r, w, fence=fence, cost=_fs(out))

    def mm(self, out, lhsT, rhs, start, stop, r, w):
        self.S.op("pe", lambda e: e.matmul(out, lhsT, rhs, start=start, stop=stop), r, w)

    def tr(self, out, in_, r, w):
        np_ = in_.shape[0]
        idn = self.ident[0:np_, 0:np_]
        self.S.op("pe", lambda e: e.transpose(out, in_, idn), list(r) + ["consts"], w)

    def ps(self, hold=False):
        i = self.ps_i
        while i in self.ps_hold:
            i = (i + 1) % 8
        self.ps_i = (i + 1) % 8
        if hold:
            self.ps_hold.add(i)
        return self.psb[i], ("ps", i)

    def qreset(self):
        self.q_i = 0

    def q(self, cols):
        n = (cols + 127) // 128
        assert self.q_i + n <= NQ, ("scratch overflow", self.q_i, n)
        i0 = self.q_i
        self.q_i += n
        self.q_max = max(getattr(self, "q_max", 0), self.q_i)
        return V(self.scr[:, i0 * 128:i0 * 128 + cols], [("q", i) for i in range(i0, i0 + n)])

    def dump(self, name, ap, keys):
        if self.dry or not self.cfg.get("dbg"):
            return
        shp = list(ap.shape)
        d = self.nc.dram_tensor("dbg_" + name, shp, ap.dtype, kind="ExternalOutput").ap()
        self.S.dma("sp", [(d, ap)], reads=keys, semres=None, is_output=True)

    def cst(self, name, rows=128):
        o, w = CO[name]
        return self.consts[0:rows, o:o + w]

    def _slot(self, sid):
        if sid < self.NSLOT:
            return self.wslot[sid], [("w", sid)]
        b0 = NQ - 16 * (sid - self.NSLOT + 1)
        return self.scr[:, b0 * 128:(b0 + 16) * 128].bitcast(BF16), [("q", b) for b in range(b0, b0 + 16)]

    def wreq(self, src, kch, ncols, pairs_fn=None):
        if self.dry:
            self.wlist.append((src, kch, ncols, pairs_fn, self.cur_ring))
            return None, None
        if self.w_i == 0:
            self.wslot_of = []
            self.wprev = []
            last = {}
            cnt = {}
            for n, g in enumerate(self.wlist):
                ring = g[4]
                c = cnt.get(ring, 0)
                cnt[ring] = c + 1
                sid = c % ring
                self.wslot_of.append(sid)
                self.wprev.append(last.get(sid, -1))
                last[sid] = n
        i = self.w_i
        self.w_i += 1
        while self.w_issued < len(self.wlist) and self.wprev[self.w_issued] < i and self.w_issued < i + 6:
            n = self.w_issued
            s_, k_, c_, pf, _ = self.wlist[n]
            st, keys = self._slot(self.wslot_of[n])
            view = st[:, 0:k_ * c_].rearrange("p (k c) -> p k c", k=k_)
            if pf is not None:
                pairs = pf(view)
            else:
                pairs = [(view, s_.rearrange("(k p) c -> p k c", p=128))]
            self.S.dma("pool", pairs, writes=keys, semres=("w", self.wslot_of[n]))
            self.w_issued += 1
        assert self.w_issued > i
        s_, k_, c_, pf, _ = self.wlist[i]
        st, keys = self._slot(self.wslot_of[i])
        return st[:, 0:k_ * c_].rearrange("p (k c) -> p k c", k=k_), keys

    def dense(self, src, kch, ncols, rhs, rkeys, n, outfn, csub=128):
        wv, wk = self.wreq(src, kch, ncols)
        if self.dry:
            return
        pieces = [(0, min(n, 512))] + ([(512, n - 512)] if n > 512 else [])
        for ci in range(ncols // csub):
            banks = [self.ps() for _ in pieces]
            for k in range(kch):
                r = rhs(k)
                for (bank, bk), (p0, w) in zip(banks, pieces):
                    self.mm(bank[0:csub, 0:w], wv[:, k, ci * csub:(ci + 1) * csub], r[:, p0:p0 + w], k == 0, k == kch - 1,
                            wk + list(rkeys(k)), [bk])
            for (bank, bk), (p0, w) in zip(banks, pieces):
                if n > 512:
                    outfn(ci, bank, bk, p0, w)
                else:
                    outfn(ci, bank, bk)

    def rms(self, src, skeys, nk, n, dim, outfn):
        pieces = [(0, min(n, 512))] + ([(512, n - 512)] if n > 512 else [])
        banks = [self.ps() for _ in pieces]
        for k in range(nk):
            sq = self.sqb[self.sq_i % 4]
            sk = ("sqb", self.sq_i % 4)
            self.sq_i += 1
            self.act(sq[:, 0:n], src(k), AF.Square, list(skeys(k)), [sk])
            for (bank, bk), (p0, w) in zip(banks, pieces):
                self.mm(bank[:, 0:w], self.ones_bf[:, :], sq[:, p0:p0 + w], k == 0, k == nk - 1, [sk, "consts2"], [bk])
        rs = self.rstd[0]
        rk = ("rstd", 0)
        for (bank, bk), (p0, w) in zip(banks, pieces):
            self.act(rs[:, p0:p0 + w], bank[:, 0:w], AF.Ln, [bk, "consts2"], [rk], scale=1.0 / dim, bias=self.epsc[:, 0:1])
        self.act(rs[:, 0:n], rs[:, 0:n], AF.Exp, [rk], [rk], scale=-0.5)
        for k in range(nk):
            outfn(k, rs[:, 0:n], rk)

    def xnorm(self, gcol, n, c0=0):
        def o(k, rs, rk):
            self.stt(self.xn[:, k, c0:c0 + n], self.xres[:, k, c0:c0 + n], gcol(k), rs, OP.mult, OP.mult,
                     [("x", k), rk, "params"], [("xn", k)])
        self.rms(lambda k: self.xres[:, k, c0:c0 + n], lambda k: [("x", k)], KD, n, D, o)

    def tok_to_fm(self, dst, dkeys, src_dram, rows, nchunk):
        mark = self.q_i
        stg = self.q(nchunk * 128)
        self.S.dma("sp", [(stg.ap[0:rows, :], src_dram)], writes=stg.keys, semres=stg.keys[0])
        for c0 in range(0, nchunk, 4):
            bank, bk = self.ps()
            nn = min(4, nchunk - c0)
            for i in range(nn):
                self.tr(bank[:, i * rows:(i + 1) * rows], stg.ap[0:rows, (c0 + i) * 128:(c0 + i + 1) * 128], stg.keys, [bk])
            for i in range(nn):
                self.cp(dst(c0 + i), bank[:, i * rows:(i + 1) * rows], [bk], dkeys(c0 + i), eng="act")
        self.q_i = mark

    def fm_to_tok(self, dst_dram, src, skeys, rows, nchunk, semres):
        mark = self.q_i
        stg = self.q(nchunk * 128)
        for c0 in range(0, nchunk, 4):
            bank, bk = self.ps()
            nn = min(4, nchunk - c0)
            for i in range(nn):
                self.tr(bank[0:rows, i * 128:(i + 1) * 128], src(c0 + i), skeys(c0 + i), [bk])
            self.cp(stg.ap[0:rows, c0 * 128:(c0 + nn) * 128], bank[0:rows, 0:nn * 128], [bk], stg.keys, eng="act")
        self.S.dma("sp", [(dst_dram, stg.ap[0:rows, :])], reads=stg.keys, semres=None, is_output=True)
        self.q_i = mark

    def build(self):
        nc, st = self.nc, self.st
        cfg = self.cfg
        self.S = S = Sch(nc, st)
        dr = {}
        for n, shp in IN_SPECS:
            dr[n] = nc.dram_tensor(n, shp, F32, kind="ExternalInput").ap()
        for n, shp in OUT_SPECS:
            dr[n] = nc.dram_tensor(n, shp, F32, kind="ExternalOutput").ap()
        self.dr = dr
        sb = lambda name, shape, dt: st.enter_context(nc.sbuf_tensor(name, shape, dt))
        self.psb = [st.enter_context(nc.psum_tensor("ps%d" % i, [128, 512], F32)) for i in range(8)]
        self.ps_i = 0
        self.consts = sb("consts_sb", [128, NCONST], F32)
        self.ident = self.cst("ident")
        self.ones_bf = sb("ones_bf", [128, 128], BF16)
        self.id3_bf = sb("id3_bf", [96, 32], BF16)
        self.epsc = sb("epsc", [128, 2], F32)
        self.xres = sb("xres", [128, KD, NW], F32)
        self.xn = sb("xn", [128, KD, NW], BF16)
        self.sqb = [sb("sqb%d" % i, [128, NW], BF16) for i in range(4)]
        self.sq_i = 0
        self.rstd = [sb("rstd%d" % i, [128, NW], F32) for i in range(1)]
        self.rs_i = 0
        self.NSLOT = 3
        self.wslot = [sb("wslot%d" % i, [128, 4096], BF16) for i in range(self.NSLOT)]
        self.scr = sb("scr", [128, NQ * 128], F32)
        P = {}
        P["g_mix"] = sb("p_gmix", [128, 4, 8], F32)
        P["g_ffn"] = sb("p_gffn", [128, 4, 8], F32)
        P["g_ple"] = sb("p_gple", [128, 4, 8], F32)
        P["g_final"] = sb("p_gfin", [128, 8], F32)
        P["lb"] = sb("p_lb", [128, 2, 4], F32)
        P["oml"] = sb("p_oml", [128, 2, 4], F32)
        P["hgn"] = sb("p_hgn", [128, 2, 4], F32)
        P["lcw"] = sb("p_lcw", [128, 2, 4, 4], F32)
        P["lcb"] = sb("p_lcb", [128, 2, 4], F32)
        P["lwa"] = sb("p_lwa", [128, 2, 4, 128], BF16)
        P["lwx"] = sb("p_lwx", [128, 2, 4, 128], BF16)
        P["lba"] = sb("p_lba", [128, 2, 4], F32)
        P["lbx"] = sb("p_lbx", [128, 2, 4], F32)
        P["lam"] = sb("p_lam", [128, 2, 4], F32)
        P["sp8"] = sb("p_sp8", [128, 2, 4], F32)
        P["sp16"] = sb("p_sp16", [128, 2, 4], F32)
        P["scw"] = sb("p_scw", [128, 2, 4, 24], F32)
        P["scb"] = sb("p_scb", [128, 2, 24], F32)
        P["dtb"] = sb("p_dtb", [96, 2], F32)
        P["A3"] = sb("p_A3", [96, 2], F32)
        P["Dp"] = sb("p_Dp", [128, 2, 16], F32)
        P["sgn"] = sb("p_sgn", [128, 2, 16], F32)
        self.P = P
        self.hgS = [sb("hgS%d" % j, [128, 4, 128], F32) for j in range(2)]
        self.hgSbf = sb("hgSbf", [128, 3, 4, 128], BF16)
        self.lruh = [sb("lruh%d" % j, [128, 4], F32) for j in range(2)]
        self.lruc = [sb("lruc%d" % j, [128, 4, 3], F32) for j in range(2)]
        self.ssmST = [sb("ssmST%d" % j, [128, 2048], F32) for j in range(2)]
        self.ssmSTbf = sb("ssmSTbf", [128, 2048], BF16)
        self.ssmc = [sb("ssmc%d" % j, [128, 24, 3], F32) for j in range(2)]

        for dry in (True, False):
            self.dry = dry
            if dry:
                self.wlist = []
                self.real_S = self.S
                self.S = _NullSch()
            else:
                self.S = self.real_S
                self.w_i = 0
                self.w_issued = 0
            self.ps_i = 0
            self.ps_hold = set()
            self.sq_i = 0
            self.rs_i = 0
            self.program()
        self.S.finish()
        return nc

    def program(self):
        S, dr, P = self.S, self.dr, self.P
        cfg = self.cfg
        S.dma("sp", [(self.consts[:], dr["consts"])], writes=["consts"], semres="consts")
        S.op("dve", lambda e: e.memset(self.ones_bf[:], 1.0), [], ["consts2"])
        S.op("dve", lambda e: e.memset(self.epsc[:, 0:1], EPS), [], ["consts2"])
        S.op("dve", lambda e: e.memset(self.epsc[:, 1:2], 1.0), [], ["consts2"])
        self.cp(self.id3_bf[:], self.cst("id3", 96), ["consts"], ["consts2"])
        self.qreset()
        if not self.dry:
            for nm in ("g_mix", "g_ffn", "g_ple"):
                self.tok_to_fm(lambda c, nm=nm: P[nm][:, :, c], lambda c: ["params"], dr[nm], 4, 8)
            self.tok_to_fm(lambda c: P["g_final"][:, c:c + 1], lambda c: ["params"], dr["g_final"].rearrange("(o d) -> o d", o=1), 1, 8)
            self.tok_to_fm(lambda c: P["lb"][:, :, c], lambda c: ["params"], dr["hgrn_lb"], 2, 4)
            self.tok_to_fm(lambda c: P["hgn"][:, :, c], lambda c: ["params"], dr["hgrn_gnorm"], 2, 4)
            self.tok_to_fm(lambda c: P["lam"][:, :, c], lambda c: ["params"], dr["lru_lam"], 2, 4)
        prs = []
        for r3 in range(3):
            prs.append((P["A3"][r3 * 32:(r3 + 1) * 32, :], dr["ssm_a_log"].rearrange("j h -> h j")))
        S.dma("sp", prs, writes=["paramsA3"], semres="params")
        prs = []
        for j in range(2):
            prs.append((P["lcw"][:, j], dr["lru_conv_w"][j].rearrange("k (c p) -> p k c", p=128)))
        prs.append((P["lcb"][:], dr["lru_conv_b"].rearrange("j (c p) -> p j c", p=128)))
        prs.append((P["lba"][:], dr["lru_ba"].rearrange("j h o -> o j h")))
        prs.append((P["lbx"][:], dr["lru_bx"].rearrange("j h o -> o j h")))
        for j in range(2):
            prs.append((P["scw"][:, j], dr["ssm_conv_w"][j].rearrange("k (c p) -> p k c", p=128)))
        prs.append((P["scb"][:], dr["ssm_conv_b"].rearrange("j (c p) -> p j c", p=128)))
        prs.append((P["sgn"][:], dr["ssm_gnorm"].rearrange("j (c p) -> p j c", p=128)))
        for r3 in range(3):
            prs.append((P["dtb"][r3 * 32:(r3 + 1) * 32, :], dr["ssm_dt_bias"].rearrange("j h -> h j")))
        for j in range(2):
            for hh in range(2):
                srcd = dr["ssm_d"][j].rearrange("(q t) -> t q", t=2)[hh:hh + 1, :]
                prs.append((P["Dp"][hh * 64:(hh + 1) * 64, j, :], srcd.broadcast_to([64, 16])))
        S.dma("act", prs, writes=["paramsB"], semres="paramsB")
        S.dma("pool", [(P["lwa"][:], dr["lru_wa"].rearrange("j h i o -> i j h o")),
                       (P["lwx"][:], dr["lru_wx"].rearrange("j h i o -> i j h o"))], writes=["params_bf"], semres="params_bf")
        tq = self.epsc
        self.qreset()
        tmp = self.q(64)
        t = tmp.ap
        self.act(t[:, 0:8], P["lb"][:].rearrange("p j h -> p (j h)"), AF.Exp, ["params"], tmp.keys)
        self.tt(t[:, 8:12], t[:, 0:4], t[:, 4:8], OP.add, tmp.keys, tmp.keys)
        S.op("dve", lambda e: e.reciprocal(t[:, 8:12], t[:, 8:12]), tmp.keys, tmp.keys)
        S.op("dve", lambda e: e.memset(P["oml"][:, 0, :], 1.0), [], ["params2"])
        self.tt(P["oml"][:, 1, :], t[:, 0:4], t[:, 8:12], OP.mult, tmp.keys, ["params2"])
        self.act(t[:, 16:24], P["lam"][:].rearrange("p j h -> p (j h)"), AF.Exp, ["params"], tmp.keys, scale=-1.0)
        self.act(t[:, 24:32], t[:, 16:24], AF.Ln, tmp.keys, tmp.keys, bias=self.epsc[:, 1:2])
        self.ts(P["sp8"][:].rearrange("p j h -> p (j h)"), t[:, 24:32], -8.0, None, OP.mult, None, tmp.keys, ["params2"])
        self.ts(P["sp16"][:].rearrange("p j h -> p (j h)"), t[:, 24:32], -16.0, None, OP.mult, None, tmp.keys, ["params2"])
        self.act(P["A3"][:], P["A3"][:], AF.Exp, ["paramsA3"], ["params2"])
        self.ts(P["A3"][:], P["A3"][:], -1.0, None, OP.mult, None, ["params2"], ["params2"])

        tiles = cfg.get("tiles", [0, 1, 2, 3])
        for tl in tiles:
            self.run_tile(tl)

    def run_tile(self, tl):
        S, dr, P = self.S, self.dr, self.P
        cfg = self.cfg
        parts = [(tl, 0, 512)]
        if tl == 0 and cfg.get("with_s", True):
            parts = [("S", 512, 128)] + parts
        if tl == "S":
            parts = [("S", 0, 128)]
        ntot = sum(p[2] for p in parts)
        self.cur_ring = 3
        self.qreset()
        for (pt, c0, n) in parts:
            xsrc = dr["xs"] if pt == "S" else dr["xp"][pt * 512:(pt + 1) * 512, :]
            for b in range(n // 128):
                stg = self.q(1024)
                S.dma("sp", [(stg.ap[:, :], xsrc[b * 128:(b + 1) * 128, :])], writes=stg.keys, semres=stg.keys[0])
                for k0 in range(0, KD, 4):
                    bank, bk = self.ps()
                    for i in range(4):
                        self.tr(bank[:, i * 128:(i + 1) * 128], stg.ap[:, (k0 + i) * 128:(k0 + i + 1) * 128], stg.keys, [bk])
                    self.cp(self.xres[:, k0:k0 + 4, c0 + b * 128:c0 + (b + 1) * 128], bank[:].rearrange("p (k t) -> p k t", k=4), [bk],
                            [("x", k0 + i) for i in range(4)], eng="act")
        for i in range(cfg.get("depth", DEPTH)):
            j = i // 2
            for (pt, c0, n) in parts:
                self.qreset()
                if i % 2 == 0:
                    if cfg.get("even", True):
                        self.even_mixer(pt, i, j, n, c0)
                else:
                    if cfg.get("odd", True):
                        self.odd_mixer(pt, i, j, n, c0)
            self.qreset()
            hold = {}
            if cfg.get("ple", True):
                hold["pT"] = [self.q(ntot // 2) for _ in range(2)]
            if cfg.get("ffn", True):
                self.ffn(i, ntot, pre=(lambda: self.ple_pre(parts, i, ntot, hold["pT"])) if "pT" in hold else None)
            elif "pT" in hold:
                self.ple_pre(parts, i, ntot, hold["pT"])
            if cfg.get("ple", True):
                self.ple(parts, i, ntot, hold["pT"])
        self.qreset()
        n = ntot
        yf = [self.q(n) for k in range(KD)]

        def o(k, rs, rk):
            self.stt(yf[k].ap, self.xres[:, k, 0:n], P["g_final"][:, k:k + 1], rs, OP.mult, OP.mult,
                     [("x", k), rk, "params"], yf[k].keys)
        self.rms(lambda k: self.xres[:, k, 0:n], lambda k: [("x", k)], KD, n, D, o)
        for (pt, c0, np_) in parts:
            ydst = dr["ys"] if pt == "S" else dr["yp"][pt * 512:(pt + 1) * 512, :]
            stgs = [self.q(1024) for _ in range(2)]
            for b in range(np_ // 128):
                stg = stgs[b % 2]
                for k0 in range(0, KD, 4):
                    bank, bk = self.ps()
                    for i in range(4):
                        self.tr(bank[:, i * 128:(i + 1) * 128], yf[k0 + i].ap[:, c0 + b * 128:c0 + (b + 1) * 128], yf[k0 + i].keys, [bk])
                    self.cp(stg.ap[:, k0 * 128:(k0 + 4) * 128], bank[:], [bk], stg.keys, eng="act")
                S.dma("sp", [(ydst[b * 128:(b + 1) * 128, :], stg.ap[:, :])], reads=stg.keys, semres=stg.keys[0], is_output=True)

    def ffn(self, i, n, pre=None):
        S, dr, P = self.S, self.dr, self.P
        self.xnorm(lambda k: P["g_ffn"][:, i, k:k + 1], n)
        if pre is not None:
            pre()
        hT = [self.q(n // 2) for f in range(KFF)]
        sil = [self.q(n) for _ in range(2)]
        si = [0]
        w1, w3, w2 = dr["ffn_w1"][i], dr["ffn_w3"][i], dr["ffn_w2"][i]
        rhs = lambda k: self.xn[:, k, 0:n]
        rk = lambda k: [("xn", k)]
        for c0 in range(0, DFF, 256):
            held = {}

            def o1(ci, bank, bk, p0=0, w=n, held=held):
                if p0 == 0:
                    held[ci] = sil[si[0] % 2]
                    si[0] += 1
                sv = held[ci]
                self.act(sv.ap[:, p0:p0 + w], bank[:, 0:w], AF.Silu, [bk], sv.keys)

            def o3(ci, bank, bk, p0=0, w=n, c0=c0, held=held):
                f = c0 // 128 + ci
                self.tt(hT[f].bf[:, p0:p0 + w], held[ci].ap[:, p0:p0 + w], bank[:, 0:w], OP.mult, [bk] + held[ci].keys, hT[f].keys)
            self.dense(w1[:, c0:c0 + 256], KD, 256, rhs, rk, n, o1)
            self.dense(w3[:, c0:c0 + 256], KD, 256, rhs, rk, n, o3)
        for c0 in range(0, D, 128):
            def o2(ci, bank, bk, p0=0, w=n, c0=c0):
                dc = c0 // 128 + ci
                self.tt(self.xres[:, dc, p0:p0 + w], self.xres[:, dc, p0:p0 + w], bank[:, 0:w], OP.add, [bk, ("x", dc)], [("x", dc)])
            self.dense(w2[:, c0:c0 + 128], KFF, 128, lambda k: hT[k].bf[:, 0:n], lambda k: hT[k].keys, n, o2)

    def ple_pre(self, parts, i, n, pT):
        S, dr, P = self.S, self.dr, self.P
        if not self.dry:
            stgs = [self.q(256) for _ in range(2)]
            bi = 0
            for (pt, c0, np_) in parts:
                psrc = dr["psm"][i] if pt == "S" else dr["pp"][i, pt * 512:(pt + 1) * 512, :]
                for b in range(np_ // 128):
                    stg = stgs[bi % 2]
                    bi += 1
                    S.dma("sp", [(stg.ap, psrc[b * 128:(b + 1) * 128, :])], writes=stg.keys, semres=stg.keys[0])
                    bank, bk = self.ps()
                    for c in range(2):
                        self.tr(bank[:, c * 128:(c + 1) * 128], stg.ap[:, c * 128:(c + 1) * 128], stg.keys, [bk])
                    for c in range(2):
                        self.cp(pT[c].bf[:, c0 + b * 128:c0 + (b + 1) * 128], bank[:, c * 128:(c + 1) * 128], [bk], pT[c].keys, eng="act")

    def ple(self, parts, i, n, pT):
        S, dr, P = self.S, self.dr, self.P
        if not self.dry:
            for k in range(KD):
                self.cp(self.xn[:, k, 0:n], self.xres[:, k, 0:n], [("x", k)], [("xn", k)], eng=("act" if k % 2 == 0 else "dve"))
        gate = [self.q(n) for _ in range(KD)]
        for c0 in range(0, D, 512):
            def og(ci, bank, bk, p0=0, w=n, c0=c0):
                dc = c0 // 128 + ci
                self.act(gate[dc].ap[:, p0:p0 + w], bank[:, 0:w], AF.Sigmoid, [bk], gate[dc].keys)
            self.dense(dr["ple_gate"][i][:, c0:c0 + 512], KD, 512, lambda k: self.xn[:, k, 0:n], lambda k: [("xn", k)], n, og)
        for c0 in range(0, D, 512):
            def oe(ci, bank, bk, p0=0, w=n, c0=c0):
                dc = c0 // 128 + ci
                self.tt(gate[dc].ap[:, p0:p0 + w], gate[dc].ap[:, p0:p0 + w], bank[:, 0:w], OP.mult, [bk] + gate[dc].keys, gate[dc].keys)
            self.dense(dr["ple_up"][i][:, c0:c0 + 512], 2, 512, lambda k: pT[k].bf[:, 0:n], lambda k: pT[k].keys, n, oe)
        if self.dry:
            return
        tmp = [self.q(n) for _ in range(2)]

        def o(k, rs, rk):
            tv = tmp[k % 2]
            self.stt(tv.ap, gate[k].ap, P["g_ple"][:, i, k:k + 1], rs, OP.mult, OP.mult, gate[k].keys + [rk, "params"], tv.keys)
            self.tt(self.xres[:, k, 0:n], self.xres[:, k, 0:n], tv.ap, OP.add, tv.keys + [("x", k)], [("x", k)])
        self.rms(lambda k: gate[k].ap, lambda k: gate[k].keys, KD, n, D, o)

    def even_mixer(self, tl, i, j, n, c0x=0):
        S, dr, P = self.S, self.dr, self.P
        samp = (tl == "S")
        nblk = n // 128
        L = LS if samp else 64
        nsb = 128 // L
        nseg = n // L
        W = dr["w_even_in"][j]
        self.xnorm(lambda k: P["g_mix"][:, i, k:k + 1], n, c0x)
        rhs = lambda k: self.xn[:, k, c0x:c0x + n]
        rk = lambda k: [("xn", k)]
        A = [self.q(n) for _ in range(4)]
        B = [self.q(n) for _ in range(4)]
        C = [self.q(n) for _ in range(4)]
        Dq = [self.q(n) for _ in range(4)]
        E = [self.q(n) for _ in range(4)]
        KDf = [self.q(n) for _ in range(4)]
        QT = [self.q(n // 2) for _ in range(4)]
        KT = [self.q(n // 2) for _ in range(4)]
        VT = [self.q(256) for _ in range(nblk)]
        KDT = [self.q(256) for _ in range(nblk)]
        act8 = [self.q(n // 2) for _ in range(8)]
        rsname = "rs8" if samp else "rs64"
        rmask = self.cst(rsname)[:, 0:n]
        def of(hc, bank, bk):
            self.act(A[hc].ap, bank[:, 0:n], AF.Sigmoid, [bk], A[hc].keys, scale=-1.0)
        self.dense(W[:, 512:1024], KD, 512, rhs, rk, n, of)
        if not self.dry:
            H4 = range(4)
            for hc in H4:
                self.ts(A[hc].ap, A[hc].ap, P["oml"][:, j, hc:hc + 1], None, OP.mult, None, A[hc].keys + ["params2"], A[hc].keys)
            for hc in H4:
                self.act(B[hc].ap, A[hc].ap, AF.Ln, A[hc].keys + ["consts2"], B[hc].keys, scale=-1.0, bias=self.epsc[:, 1:2])
            for hc in H4:
                S.op("dve", lambda e, hc=hc: e.tensor_tensor_scan(out=C[hc].ap, data0=rmask, data1=B[hc].ap, initial=0.0,
                                                                  op0=OP.mult, op1=OP.add), B[hc].keys + ["consts"], C[hc].keys, cost=2 * n)
            for hc in H4:
                self.act(B[hc].ap, C[hc].ap, AF.Exp, C[hc].keys, B[hc].keys)
                self.act(Dq[hc].ap, C[hc].ap, AF.Exp, C[hc].keys, Dq[hc].keys, scale=-1.0)
            for hc in H4:
                self.tt(Dq[hc].ap, A[hc].ap, Dq[hc].ap, OP.mult, A[hc].keys + Dq[hc].keys, Dq[hc].keys)
            for hc in H4:
                self.cp(KT[hc].bf[:, 0:n], Dq[hc].ap, Dq[hc].keys, KT[hc].keys, eng="act")
            for hc in H4:
                ebl = B[hc].ap.rearrange("p (s l) -> p s l", l=L)[:, :, L - 1:L].broadcast_to([128, nseg, L])
                self.tt(KDf[hc].ap.rearrange("p (s l) -> p s l", l=L), Dq[hc].ap.rearrange("p (s l) -> p s l", l=L), ebl,
                        OP.mult, Dq[hc].keys + B[hc].keys, KDf[hc].keys)
        def oq(hc, bank, bk):
            self.act(E[hc].ap, bank[:, 0:n], AF.Silu, [bk], E[hc].keys)
        self.dense(W[:, 0:512], KD, 512, rhs, rk, n, oq)
        wv, wk = self.wreq(W[:, 1024:1536], KD, 512)
        if not self.dry:
            for b in range(nblk):
                bank, bk = self.ps()
                for k in range(KD):
                    self.mm(bank[:, :], self.xn[:, k, c0x + b * 128:c0x + (b + 1) * 128], wv[:, k, :], k == 0, k == KD - 1, wk + [("xn", k)], [bk])
                self.cp(VT[b].bf[:, 0:512], bank[:, :], [bk], VT[b].keys, eng="act")
        if not self.dry:
            for hc in range(4):
                self.tt(QT[hc].bf[:, 0:n], E[hc].ap, B[hc].ap, OP.mult, E[hc].keys + B[hc].keys, QT[hc].keys)
        if not self.dry:
            for b in range(nblk):
                bank, bk = self.ps()
                for hc in range(4):
                    self.tr(bank[:, hc * 128:(hc + 1) * 128], KDf[hc].ap[:, b * 128:(b + 1) * 128], KDf[hc].keys, [bk])
                self.cp(KDT[b].bf[:, 0:512], bank[:, :], [bk], KDT[b].keys, eng="act")
        YB = A
        def oy(cc, bank, bk):
            self.cp(YB[cc].ap, bank[:, 0:n], [bk], YB[cc].keys, eng="act")
        self.dense(W[:, 2048:2560], KD, 512, rhs, rk, n, oy)
        if samp:
            ext = [self.q(16 * 11) for _ in range(4)]
            extv = [x.ap.rearrange("p (s t) -> p s t", t=11) for x in ext]
        else:
            ext = [self.q(n + 3) for _ in range(4)]
        if samp and not self.dry:
            cin = self.q(4 * 48)
            cinv = cin.ap.rearrange("p (c r) -> p c r", c=4)
            self.tok_to_fm(lambda c: cinv[:, c, :], lambda c: cin.keys, dr["st_lru_conv"][j], 48, 4)
            h0 = self.q(4 * 16)
            h0v = h0.ap.rearrange("p (c r) -> p c r", c=4)
            self.tok_to_fm(lambda c: h0v[:, c, :], lambda c: h0.keys, dr["st_lru_h"][j], 16, 4)
        U = KDf
        def ou(cc, bank, bk):
            if samp:
                self.cp(extv[cc][:, :, 3:11], bank[:, 0:n].rearrange("p (s t) -> p s t", t=8), [bk], ext[cc].keys, eng="act")
                self.cp(extv[cc][:, :, 0:3], cinv[:, cc, :].rearrange("p (s r) -> p s r", r=3), cin.keys, ext[cc].keys)
                taps = [extv[cc][:, :, k:k + 8] for k in range(4)]
                uo = U[cc].ap.rearrange("p (s t) -> p s t", t=8)
            else:
                self.cp(ext[cc].ap[:, 3:3 + n], bank[:, 0:n], [bk], ext[cc].keys, eng="act")
                if tl == 0:
                    S.op("dve", lambda e: e.memset(ext[cc].ap[:, 0:3], 0.0), [], ext[cc].keys)
                else:
                    self.cp(ext[cc].ap[:, 0:3], self.lruc[j][:, cc, :], [("lruc", j, cc)], ext[cc].keys)
                self.cp(self.lruc[j][:, cc, :], ext[cc].ap[:, n:n + 3], ext[cc].keys, [("lruc", j, cc)])
                taps = [ext[cc].ap[:, k:k + n] for k in range(4)]
                uo = U[cc].ap
            self.act(U[cc].ap, bank[:, 0:n], AF.Identity, [bk, "paramsB"], U[cc].keys, scale=P["lcw"][:, j, 3, cc:cc + 1], bias=P["lcb"][:, j, cc:cc + 1])
            for k in range(0, 3):
                self.stt(uo, taps[k], P["lcw"][:, j, k, cc:cc + 1], uo, OP.mult, OP.add, ext[cc].keys + U[cc].keys + ["paramsB"], U[cc].keys)
        self.dense(W[:, 2560:3072], KD, 512, rhs, rk, n, ou)
        G = Dq
        def og(hc, bank, bk):
            self.act(G[hc].ap, bank[:, 0:n], AF.Silu, [bk], G[hc].keys)
        if self.dry:
            self.dense(W[:, 1536:2048], KD, 512, rhs, rk, n, og)
            wv, wk = self.wreq(dr["w_even_out"][j][:, 0:512], KD, 512)
            wv, wk = self.wreq(dr["w_even_out"][j][:, 512:1024], KD, 512)
            return
        Ub = [self.q(n // 2) for _ in range(4)]
        R = C
        GI = Dq
        AA = E
        MM = U
        C4 = range(4)

        def lru1():
            for cc in C4:
                self.cp(Ub[cc].bf[:, 0:n], U[cc].ap, U[cc].keys, Ub[cc].keys, eng="act")
            for cc in C4:
                bank, bk = self.ps()
                self.mm(bank[:, 0:n], P["lwa"][:, j, cc, :], Ub[cc].bf[:, 0:n], True, True, Ub[cc].keys + ["params_bf"], [bk])
                self.act(R[cc].ap, bank[:, 0:n], AF.Sigmoid, [bk, "paramsB"], R[cc].keys, bias=P["lba"][:, j, cc:cc + 1])
                bank, bk = self.ps()
                self.mm(bank[:, 0:n], P["lwx"][:, j, cc, :], Ub[cc].bf[:, 0:n], True, True, Ub[cc].keys + ["params_bf"], [bk])
                self.act(GI[cc].ap, bank[:, 0:n], AF.Sigmoid, [bk, "paramsB"], GI[cc].keys, bias=P["lbx"][:, j, cc:cc + 1])

        def lru2():
            for cc in C4:
                self.act(AA[cc].ap, R[cc].ap, AF.Exp, R[cc].keys + ["params2"], AA[cc].keys, scale=P["sp8"][:, j, cc:cc + 1])
            for cc in C4:
                self.tt(GI[cc].ap, GI[cc].ap, U[cc].ap, OP.mult, GI[cc].keys + U[cc].keys, GI[cc].keys)
            for cc in C4:
                self.act(MM[cc].ap, R[cc].ap, AF.Exp, R[cc].keys + ["params2"], MM[cc].keys, scale=P["sp16"][:, j, cc:cc + 1])
            for cc in C4:
                self.act(MM[cc].ap, MM[cc].ap, AF.Sqrt, MM[cc].keys + ["consts2"], MM[cc].keys, scale=-1.0, bias=self.epsc[:, 1:2])
            if (not samp) and tl == 0:
                for cc in C4:
                    S.op("dve", lambda e, cc=cc: e.memset(MM[cc].ap[:, 0:1], 1.0), MM[cc].keys, MM[cc].keys)
                    S.op("dve", lambda e, cc=cc: e.memset(self.lruh[j][:, cc:cc + 1], 0.0), [], [("lruh", j, cc)])
            for cc in C4:
                self.tt(GI[cc].ap, GI[cc].ap, MM[cc].ap, OP.mult, GI[cc].keys + MM[cc].keys, GI[cc].keys, fence=True)

        def lru3():
            for cc in C4:
                if samp:
                    for sq in range(16):
                        S.op("dve", lambda e, cc=cc, sq=sq: e.tensor_tensor_scan(
                            out=R[cc].ap[:, sq * 8:(sq + 1) * 8], data0=AA[cc].ap[:, sq * 8:(sq + 1) * 8], data1=GI[cc].ap[:, sq * 8:(sq + 1) * 8],
                            initial=h0v[:, cc, sq:sq + 1], op0=OP.mult, op1=OP.add), AA[cc].keys + GI[cc].keys + h0.keys, R[cc].keys, fence=(sq == 0))
                else:
                    S.op("dve", lambda e, cc=cc: e.tensor_tensor_scan(
                        out=R[cc].ap, data0=AA[cc].ap, data1=GI[cc].ap, initial=self.lruh[j][:, cc:cc + 1], op0=OP.mult, op1=OP.add),
                        AA[cc].keys + GI[cc].keys + [("lruh", j, cc)], R[cc].keys, fence=True, cost=2 * n)
            if not samp:
                for cc in C4:
                    self.cp(self.lruh[j][:, cc:cc + 1], R[cc].ap[:, n - 1:n], R[cc].keys, [("lruh", j, cc)], fence=True)

        def lru4():
            for cc in C4:
                self.act(AA[cc].ap, YB[cc].ap, AF.Square, YB[cc].keys, AA[cc].keys)
            for cc in C4:
                self.ts(AA[cc].ap, AA[cc].ap, 0.044715, 1.0, OP.mult, OP.add, AA[cc].keys, AA[cc].keys)
                self.tt(AA[cc].ap, AA[cc].ap, YB[cc].ap, OP.mult, AA[cc].keys + YB[cc].keys, AA[cc].keys)
            for cc in C4:
                self.act(AA[cc].ap, AA[cc].ap, AF.Sigmoid, AA[cc].keys, AA[cc].keys, scale=1.5957691216057308)
            for cc in C4:
                self.tt(AA[cc].ap, AA[cc].ap, YB[cc].ap, OP.mult, AA[cc].keys + YB[cc].keys, AA[cc].keys)
                self.tt(act8[4 + cc].bf[:, 0:n], AA[cc].ap, R[cc].ap, OP.mult, AA[cc].keys + R[cc].keys, act8[4 + cc].keys)

        lru_stages = [lru1, lru2, lru3, lru4]
        if True:
            if samp:
                S0 = self.q(64 * 128)
                S0v = S0.ap.rearrange("p (s h e) -> p s h e", s=16, h=4)
                S.dma("sp", [(S0v[:, sq], dr["st_hgrn"][j, sq].rearrange("h d e -> d h e")) for sq in range(16)],
                      writes=S0.keys, semres=S0.keys[0])
                S0bf = self.q(32 * 128)
                S0bv = S0bf.bf.rearrange("p (s h e) -> p s h e", s=16, h=4)
                self.cp(S0bf.bf, S0.ap, S0.keys, S0bf.keys, eng="act")
                Snew = S0
                Snv = S0v
                rmk = self.cst("rm16")
                mname = "mbd8"
            else:
                hgS = self.hgS[j]
                if tl == 0:
                    S.op("dve", lambda e: e.memset(hgS[:], 0.0), [], [("hgS", j)])
                rmk = self.cst("rm2")
                mname = "mbd64"
            mask = self.cst(mname)
            SCq = [self.q(256) for _ in range(2)]
            VTm = [self.q(256) for _ in range(2)]
            F = [self.q(n) for _ in range(4)]
            vi = 0
            for b in range(nblk):
                if not samp:
                    self.cp(self.hgSbf[:, 0], hgS[:], [("hgS", j)], [("hgSbf", 0)], eng="act")
                bank, bk = self.ps()
                for hc in range(4):
                    self.mm(bank[:, hc * 128:(hc + 1) * 128], KT[hc].bf[:, b * 128:(b + 1) * 128], QT[hc].bf[:, b * 128:(b + 1) * 128],
                            True, True, KT[hc].keys + QT[hc].keys, [bk])
                sc = SCq[b % 2]
                self.tt(sc.bf[:, 0:512].rearrange("p (h t) -> p h t", h=4), bank[:, :].rearrange("p (h t) -> p h t", h=4),
                        mask.unsqueeze(1).broadcast_to([128, 4, 128]), OP.mult, [bk, "consts"], sc.keys)
                if nsb == 2:
                    vms = []
                    for sg in range(nsb):
                        vm = VTm[sg]
                        self.ts(vm.bf[:, 0:512], VT[b].bf[:, 0:512], rmk[:, sg:sg + 1], None, OP.mult, None, VT[b].keys + ["consts"], vm.keys)
                        vms.append(vm)
                for sg in range(nsb):
                    gseg = b * nsb + sg
                    if nsb == 2:
                        vm = vms[sg]
                    else:
                        vm = VTm[vi % 2]
                        vi += 1
                        self.ts(vm.bf[:, 0:512], VT[b].bf[:, 0:512], rmk[:, sg:sg + 1], None, OP.mult, None, VT[b].keys + ["consts"], vm.keys)
                    bank, bk = self.ps()
                    for hc in range(4):
                        self.mm(bank[:, hc * 128:(hc + 1) * 128], KDT[b].bf[:, hc * 128:(hc + 1) * 128], vm.bf[:, hc * 128:(hc + 1) * 128],
                                True, True, KDT[b].keys + vm.keys, [bk])
                    for hc in range(4):
                        ebl = B[hc].ap[:, gseg * L + L - 1:gseg * L + L]
                        if samp:
                            self.stt(Snv[:, sg, hc, :], S0v[:, sg, hc, :], ebl, bank[:, hc * 128:(hc + 1) * 128], OP.mult, OP.add,
                                     S0.keys + B[hc].keys + [bk], Snew.keys)
                        else:
                            self.stt(hgS[:, hc, :], hgS[:, hc, :], ebl, bank[:, hc * 128:(hc + 1) * 128], OP.mult, OP.add,
                                     [("hgS", j), bk] + B[hc].keys, [("hgS", j)])
                    if not samp:
                        self.cp(self.hgSbf[:, sg + 1], hgS[:], [("hgS", j)], [("hgSbf", sg + 1)], eng="act")
                bank, bk = self.ps()
                for hc in range(4):
                    self.mm(bank[:, hc * 128:(hc + 1) * 128], VT[b].bf[:, hc * 128:(hc + 1) * 128], sc.bf[:, hc * 128:(hc + 1) * 128],
                            True, False, VT[b].keys + sc.keys, [bk])
                    for sg in range(nsb):
                        c0 = b * 128 + sg * L
                        if samp:
                            lh = S0bv[:, sg, hc, :]
                            lk = S0bf.keys
                        else:
                            lh = self.hgSbf[:, sg, hc, :]
                            lk = [("hgSbf", sg)]
                        self.mm(bank[:, hc * 128 + sg * L:hc * 128 + (sg + 1) * L], lh, QT[hc].bf[:, c0:c0 + L],
                                False, sg == nsb - 1, lk + QT[hc].keys, [bk])
                for hc in range(4):
                    self.cp(F[hc].ap[:, b * 128:(b + 1) * 128], bank[:, hc * 128:(hc + 1) * 128], [bk], F[hc].keys, eng="act")
                if b < len(lru_stages):
                    lru_stages[b]()
            for st_ in lru_stages[nblk:]:
                st_()
            if samp:
                S.dma("sp", [(dr["hg_s"][j, sq].rearrange("h d e -> d h e"), Snv[:, sq]) for sq in range(16)],
                      reads=Snew.keys, semres=Snew.keys[0], is_output=True)
            elif tl == self.cfg.get("last", 3):
                S.dma("sp", [(dr["hg_p"][j].rearrange("h d e -> d h e"), hgS[:])], reads=[("hgS", j)], semres=("hgS", j), is_output=True)
        self.dense(W[:, 1536:2048], KD, 512, rhs, rk, n, og)
        for hc in range(4):
            def o(k, rs, rkk, hc=hc):
                self.stt(F[hc].ap, F[hc].ap, P["hgn"][:, j, hc:hc + 1], rs, OP.mult, OP.mult, F[hc].keys + [rkk, "params"], F[hc].keys)
                self.tt(act8[hc].bf[:, 0:n], F[hc].ap, G[hc].ap, OP.mult, F[hc].keys + G[hc].keys, act8[hc].keys)
            self.rms(lambda k, hc=hc: F[hc].ap, lambda k, hc=hc: F[hc].keys, 1, n, 128, o)
        if samp:
            hl = self.q(64)
            hlv = hl.ap.rearrange("p (c s) -> p c s", c=4)
            for cc in range(4):
                self.cp(hlv[:, cc, :], R[cc].ap[:, 7:128:8], R[cc].keys, hl.keys, fence=True)
            self.fm_to_tok(dr["lh_s"][j], lambda c: hlv[:, c, :], lambda c: hl.keys, 16, 4, ("lh_s", j))
            cl = self.q(4 * 48)
            clv = cl.ap.rearrange("p (c s r) -> p c s r", c=4, r=3)
            for cc in range(4):
                self.cp(clv[:, cc], extv[cc][:, :, 8:11], ext[cc].keys, cl.keys)
            clf = cl.ap.rearrange("p (c x) -> p c x", c=4)
            self.fm_to_tok(dr["lc_s"][j], lambda c: clf[:, c, :], lambda c: cl.keys, 48, 4, ("lc_s", j))
        elif tl == self.cfg.get("last", 3):
            self.fm_to_tok(dr["lh_p"][j], lambda c: self.lruh[j][:, c:c + 1], lambda c: [("lruh", j, c)], 1, 4, ("lh_p", j))
            self.fm_to_tok(dr["lc_p"][j], lambda c: self.lruc[j][:, c, :], lambda c: [("lruc", j, c)], 3, 4, ("lc_p", j))
        for c0 in range(0, D, 512):
            def oo(ci, bank, bk, c0=c0):
                dc = c0 // 128 + ci
                self.tt(self.xres[:, dc, c0x:c0x + n], self.xres[:, dc, c0x:c0x + n], bank[:, 0:n], OP.add, [bk, ("x", dc)], [("x", dc)])
            self.dense(dr["w_even_out"][j][:, c0:c0 + 512], KD, 512, lambda k: act8[k].bf[:, 0:n], lambda k: act8[k].keys, n, oo)

    def odd_mixer(self, tl, i, j, n, c0x=0):
        S, dr, P = self.S, self.dr, self.P
        samp = (tl == "S")
        nblk = n // 128
        W = dr["ssm_in"][j]
        self.xnorm(lambda k: P["g_mix"][:, i, k:k + 1], n, c0x)
        rhs = lambda k: self.xn[:, k, c0x:c0x + n]
        rk = lambda k: [("xn", k)]
        XY = [self.q(n) for _ in range(16)]
        small = self.q(4 * n)
        sm = small.ap
        c3q = self.q(n // 2)
        c3 = c3q.bf[:, 0:n]
        tk = self.q(5 * nblk * 32)
        tkv = tk.ap[:, 0:5 * nblk * 32].rearrange("p (a b h) -> p a b h", a=5, b=nblk)
        negcum, cumtok, dttok, dte, elb = [tkv[:, a] for a in range(5)]
        def pf(view):
            return [(view[:, :, r3 * 32:(r3 + 1) * 32], W[:, 5120:5152].rearrange("(k p) c -> p k c", p=128)) for r3 in range(3)]
        wv, wk = self.wreq(W[:, 5120:5152], KD, 96, pf)
        if not self.dry:
            bank, bk = self.ps()
            for k in range(KD):
                self.mm(bank[0:96, 0:n], wv[:, k, :], self.xn[:, k, c0x:c0x + n], k == 0, k == KD - 1, wk + [("xn", k)], [bk])
            xs_, dt_, cum_, r1_ = sm[0:96, 0:n], sm[0:96, n:2 * n], sm[0:96, 2 * n:3 * n], sm[0:96, 3 * n:4 * n]
            sk = small.keys
            self.ts(xs_, bank[0:96, 0:n], P["dtb"][:, j:j + 1], None, OP.add, None, [bk, "paramsB"], sk)
            self.act(r1_, xs_, AF.Abs, sk, sk)
            self.act(r1_, r1_, AF.Exp, sk, sk, scale=-1.0)
            self.act(r1_, r1_, AF.Ln, sk + ["consts2"], sk, bias=self.epsc[0:96, 1:2])
            self.stt(dt_, xs_, 0.0, r1_, OP.max, OP.add, sk, sk)
            self.ts(xs_, dt_, P["A3"][:, j:j + 1], None, OP.mult, None, sk + ["params2"], sk)
            rmask = self.cst("rs8" if samp else "rs128", 96)[:, 0:n]
            S.op("dve", lambda e: e.tensor_tensor_scan(out=cum_, data0=rmask, data1=xs_, initial=0.0, op0=OP.mult, op1=OP.add),
                 sk + ["consts"], sk)
            self.cp(c3[0:96, :], cum_, sk, c3q.keys)
            self.tt(r1_[0:96, :], cum_[0:96, :], c3[0:96, :], OP.subtract, sk + c3q.keys, sk)
            self.cp(c3[32:64, :], r1_[32:64, :], sk, c3q.keys)
            self.cp(c3[64:96, :], r1_[64:96, :], sk, c3q.keys)
            self.tt(r1_[64:96, :], r1_[64:96, :], c3[64:96, :], OP.subtract, sk + c3q.keys, sk)
            self.cp(c3[64:96, :], r1_[64:96, :], sk, c3q.keys)
        BT = [self.q(n // 2) for _ in range(4)]
        CT = [self.q(n // 2) for _ in range(4)]
        BF = [self.q(n) for _ in range(4)]
        if samp and not self.dry:
            cout = self.q(24 * 48)
            coutv = cout.ap.rearrange("p (c s r) -> p c s r", c=24, r=3)
        mark_pre = self.q_i
        if samp:
            ext = [self.q(16 * 11) for _ in range(4)]
            extv = [x.ap.rearrange("p (s t) -> p s t", t=11) for x in ext]
            if not self.dry:
                cin = self.q(24 * 48)
                cinv = cin.ap.rearrange("p (c r) -> p c r", c=24)
                self.tok_to_fm(lambda c: cinv[:, c, :], lambda c: cin.keys, dr["st_ssm_conv"][j], 48, 24)
        else:
            ext = [self.q(n + 3) for _ in range(4)]
        NACC = 4
        acc = [self.q(n) for _ in range(NACC)]
        ai = [0]
        pend = []
        for gi in range(6):
            def oc(ci, bank, bk, gi=gi):
                c = gi * 4 + ci
                av = acc[ai[0] % NACC]
                ai[0] += 1
                if samp:
                    self.cp(extv[ci][:, :, 3:11], bank[:, 0:n].rearrange("p (s t) -> p s t", t=8), [bk], ext[ci].keys, eng="act")
                    self.cp(extv[ci][:, :, 0:3], cinv[:, c, :].rearrange("p (s r) -> p s r", r=3), cin.keys, ext[ci].keys)
                    self.cp(coutv[:, c], extv[ci][:, :, 8:11], ext[ci].keys, cout.keys)
                    taps = [extv[ci][:, :, k:k + 8] for k in range(4)]
                    uo = av.ap.rearrange("p (s t) -> p s t", t=8)
                else:
                    self.cp(ext[ci].ap[:, 3:3 + n], bank[:, 0:n], [bk], ext[ci].keys, eng="act")
                    if tl == 0:
                        S.op("dve", lambda e: e.memset(ext[ci].ap[:, 0:3], 0.0), [], ext[ci].keys)
                    else:
                        self.cp(ext[ci].ap[:, 0:3], self.ssmc[j][:, c, :], [("ssmc", j, c)], ext[ci].keys, eng="act")
                    self.cp(self.ssmc[j][:, c, :], ext[ci].ap[:, n:n + 3], ext[ci].keys, [("ssmc", j, c)], eng="act")
                    taps = [ext[ci].ap[:, k:k + n] for k in range(4)]
                    uo = av.ap
                self.act(av.ap, bank[:, 0:n], AF.Identity, [bk, "paramsB"], av.keys, scale=P["scw"][:, j, 3, c:c + 1], bias=P["scb"][:, j, c:c + 1])
                while pend:
                    pend.pop(0)()
                for k in range(0, 3):
                    self.stt(uo, taps[k], P["scw"][:, j, k, c:c + 1], uo, OP.mult, OP.add, ext[ci].keys + av.keys + ["paramsB"], av.keys)

                def fin(gi=gi, ci=ci, c=c, av=av):
                    if gi < 4:
                        self.act(XY[c].ap, av.ap, AF.Silu, av.keys, XY[c].keys)
                    elif gi == 4:
                        self.act(BF[ci].ap, av.ap, AF.Silu, av.keys, BF[ci].keys)
                        self.cp(BT[ci].bf[:, 0:n], BF[ci].ap, BF[ci].keys, BT[ci].keys)
                    else:
                        self.act(CT[ci].bf[:, 0:n], av.ap, AF.Silu, av.keys, CT[ci].keys)
                pend.append(fin)
            self.dense(W[:, 2048 + gi * 512:2048 + (gi + 1) * 512], KD, 512, rhs, rk, n, oc)
        while pend:
            pend.pop(0)()
        if not self.dry:
            slname = "sl8" if samp else "sl128"
            for b in range(nblk):
                bank, bk = self.ps()
                self.tr(bank[:, 0:32], cum_[0:32, b * 128:(b + 1) * 128], sk, [bk])
                self.tr(bank[:, 32:64], dt_[0:32, b * 128:(b + 1) * 128], sk, [bk])
                self.cp(cumtok[:, b, :], bank[:, 0:32], [bk], tk.keys, eng="act")
                self.cp(dttok[:, b, :], bank[:, 32:64], [bk], tk.keys, eng="act")
                self.ts(negcum[:, b, :], bank[:, 0:32], -1.0, None, OP.mult, None, [bk], tk.keys)
                bank2, bk2 = self.ps()
                self.mm(bank2[:, 0:32], self.cst(slname), cumtok[:, b, :], True, True, tk.keys + ["consts"], [bk2])
                self.act(elb[:, b, :], bank2[:, 0:32], AF.Exp, [bk2], tk.keys)
                self.tt(dte[:, b, :], bank2[:, 0:32], negcum[:, b, :], OP.add, [bk2] + tk.keys, tk.keys)
                self.act(dte[:, b, :], dte[:, b, :], AF.Exp, tk.keys, tk.keys)
        if not self.dry and j == 0:
            self.dump("xy0", XY[0].ap, XY[0].keys)
            self.dump("xy4", XY[4].ap, XY[4].keys)
            self.dump("xy8", XY[8].ap, XY[8].keys)
        self.q_i = mark_pre
        if not self.dry:
            self.ssd_scan(tl, j, n, XY, BT, CT, BF, c3, c3q, small, tk, negcum, cumtok, dttok, dte, elb,
                          (coutv, cout) if samp else None)
            self.q_i = mark_pre
        ZS = [self.q(n) for _ in range(2)]
        zi = [0]
        YN = [self.q(n // 2) for _ in range(16)]
        pendz = []
        for gi in range(4):
            def oz(ci, bank, bk, gi=gi):
                c = gi * 4 + ci
                zv = ZS[zi[0] % 2]
                zi[0] += 1
                self.act(zv.ap, bank[:, 0:n], AF.Silu, [bk], zv.keys)
                self.tt(XY[c].ap, XY[c].ap, zv.ap, OP.mult, XY[c].keys + zv.keys, XY[c].keys)
            self.dense(W[:, gi * 512:(gi + 1) * 512], KD, 512, rhs, rk, n, oz)
            if not self.dry:
                while pendz:
                    pendz.pop(0)()

                def gn(gi=gi):
                    def o(k, rs, rkk, gi=gi):
                        c = gi * 4 + k
                        self.stt(YN[c].bf[:, 0:n], XY[c].ap, P["sgn"][:, j, c:c + 1], rs, OP.mult, OP.mult, XY[c].keys + [rkk, "paramsB"], YN[c].keys)
                    self.rms(lambda k, gi=gi: XY[gi * 4 + k].ap, lambda k, gi=gi: XY[gi * 4 + k].keys, 4, n, 512, o)
                pendz.append(gn)
        while pendz:
            pendz.pop(0)()
        for c0 in range(0, D, 256):
            def oo(ci, bank, bk, c0=c0):
                dc = c0 // 128 + ci
                self.tt(self.xres[:, dc, c0x:c0x + n], self.xres[:, dc, c0x:c0x + n], bank[:, 0:n], OP.add, [bk, ("x", dc)], [("x", dc)])
            self.dense(dr["ssm_out"][j][:, c0:c0 + 256], 16, 256, lambda k: YN[k].bf[:, 0:n], lambda k: YN[k].keys, n, oo)

    def ssd_scan(self, tl, j, n, XY, BT, CT, BF, c3, c3q, small, tk, negcum, cumtok, dttok, dte, elb, coutp):
        S, dr, P = self.S, self.dr, self.P
        samp = (tl == "S")
        nblk = n // 128
        ST = self.ssmST[j]
        STbf = self.ssmSTbf
        if samp:
            coutv, cout = coutp
            cof = cout.ap.rearrange("p (c x) -> p c x", c=24)
            self.fm_to_tok(dr["sc_s"][j], lambda c: cof[:, c, :], lambda c: cout.keys, 48, 24, ("sc_s", j))
        else:
            if tl == 0:
                S.op("dve", lambda e: e.memset(ST[:], 0.0), [], [("ST", j, g) for g in range(4)])
            self.cp(STbf[:], ST[:], [("ST", j, g) for g in range(4)], [("STbf", g) for g in range(4)], eng="act")
        mask = self.cst("mbd8" if samp else "mc128")
        mark_tmp = self.q_i
        NB2 = 1 if samp else 2
        CBm_ = [self.q(256) for _ in range(NB2)]
        XDT_ = [self.q(1024) for _ in range(NB2)]
        XW_ = [self.q(1024) for _ in range(NB2)]
        BTOK_ = [self.q(256) for _ in range(NB2)]
        NR = 3
        L4 = [self.q(256) for _ in range(NR)]
        SC4 = [self.q(256) for _ in range(NR)]
        CT4 = [self.q(256) for _ in range(8 if samp else NR)]
        E4 = [self.q(512) for _ in range(NR)]
        tmpS_ = [self.q(512) for _ in range(2)]
        NC3 = self.q(n // 2)
        nc3 = NC3.bf[:, 0:n]
        if not samp or True:
            self.ts(nc3[0:96, :], c3[0:96, :], -1.0, None, OP.mult, None, c3q.keys, NC3.keys)

        def pre(b):
            bs = slice(b * 128, (b + 1) * 128)
            CBm, XDT, XW, BTOK = CBm_[b % NB2], XDT_[b % NB2], XW_[b % NB2], BTOK_[b % NB2]
            xdt3 = XDT.bf[:, 0:2048].rearrange("p (h d) -> p h d", h=32)
            xw3 = XW.bf[:, 0:2048].rearrange("p (h d) -> p h d", h=32)
            bank, bk = self.ps()
            for g in range(4):
                self.mm(bank[:, g * 128:(g + 1) * 128], BT[g].bf[:, bs], CT[g].bf[:, bs], True, True, BT[g].keys + CT[g].keys, [bk])
            self.tt(CBm.bf[:, 0:512].rearrange("p (g t) -> p g t", g=4), bank[:, :].rearrange("p (g t) -> p g t", g=4),
                    mask.unsqueeze(1).broadcast_to([128, 4, 128]), OP.mult, [bk, "consts"], CBm.keys)
            for q4 in range(4):
                bank, bk = self.ps()
                for i4 in range(4):
                    self.tr(bank[:, i4 * 128:(i4 + 1) * 128], XY[q4 * 4 + i4].ap[:, bs], XY[q4 * 4 + i4].keys, [bk])
                self.tt(xdt3[:, q4 * 8:(q4 + 1) * 8, :], bank[:, :].rearrange("p (h d) -> p h d", h=8),
                        dttok[:, b, q4 * 8:(q4 + 1) * 8].unsqueeze(2).broadcast_to([128, 8, 64]), OP.mult, [bk] + tk.keys, XDT.keys)
            self.tt(xw3, xdt3, dte[:, b, :].unsqueeze(2).broadcast_to([128, 32, 64]), OP.mult, XDT.keys + tk.keys, XW.keys, eng=PEN)
            bank, bk = self.ps()
            for g in range(4):
                self.tr(bank[:, g * 128:(g + 1) * 128], BF[g].ap[:, bs], BF[g].keys, [bk])
            self.cp(BTOK.bf[:, 0:512], bank[:, :], [bk], BTOK.keys, eng="act")

        pre(0)
        for b in range(nblk):
            bs = slice(b * 128, (b + 1) * 128)
            CBm, XDT, XW, BTOK = CBm_[b % NB2], XDT_[b % NB2], XW_[b % NB2], BTOK_[b % NB2]
            xdt3 = XDT.bf[:, 0:2048].rearrange("p (h d) -> p h d", h=32)
            if b + 1 < nblk:
                pre(b + 1)
            cur = {}

            def stageA(hg, b=b, bs=bs, CBm=CBm):
                g = hg // 2
                bankc, bkc = self.ps()
                bankd, bkd = self.ps()
                for i4 in range(4):
                    h = hg * 4 + i4
                    sel = self.id3_bf[:, h:h + 1].broadcast_to([96, 128])
                    self.mm(bankc[:, i4 * 128:(i4 + 1) * 128], sel, c3[0:96, bs], True, True, c3q.keys + ["consts2"], [bkc])
                    self.mm(bankd[:, i4 * 128:(i4 + 1) * 128], sel, c3[0:96, bs], True, False, c3q.keys + ["consts2"], [bkd])
                    self.mm(bankd[:, i4 * 128:(i4 + 1) * 128], nc3[0:96, bs], sel, False, True, NC3.keys + ["consts2"], [bkd])
                l4, s4, e4 = L4[hg % NR], SC4[hg % NR], E4[hg % NR]
                c4 = CT4[hg % len(CT4)]
                self.act(l4.bf[:, 0:512], bankd[:, :], AF.Exp, [bkd], l4.keys)
                self.act(e4.ap, bankc[:, :], AF.Exp, [bkc], e4.keys)
                self.stt(s4.bf[:, 0:512].rearrange("p (h t) -> p h t", h=4), l4.bf[:, 0:512].rearrange("p (h t) -> p h t", h=4), 1.0,
                         CBm.bf[:, g * 128:(g + 1) * 128].unsqueeze(1).broadcast_to([128, 4, 128]), OP.min, OP.mult,
                         l4.keys + CBm.keys, s4.keys)
                self.tt(c4.bf[:, 0:512].rearrange("p (h t) -> p h t", h=4), e4.ap.rearrange("p (h t) -> p h t", h=4),
                        CT[g].bf[:, bs].unsqueeze(1).broadcast_to([128, 4, 128]), OP.mult, e4.keys + CT[g].keys, c4.keys, eng=PEN)

            def stageB(hg, b=b, bs=bs, XDT=XDT, xdt3=xdt3, cur=cur):
                s4 = SC4[hg % NR]
                c4 = CT4[hg % len(CT4)]
                if hg % 2 == 0:
                    cur["y"] = self.ps()
                banky, bky = cur["y"]
                for i4 in range(4):
                    h = hg * 4 + i4
                    po = (h % 2) * 64
                    col = ((h // 2) % 4) * 128
                    self.mm(banky[po:po + 64, col:col + 128], xdt3[:, h, :], s4.bf[:, i4 * 128:(i4 + 1) * 128], True, samp,
                            XDT.keys + s4.keys, [bky])
                    if not samp:
                        self.mm(banky[po:po + 64, col:col + 128], STbf[:, h * 64:(h + 1) * 64], c4.bf[:, i4 * 128:(i4 + 1) * 128], False, True,
                                [("STbf", h // 8)] + c4.keys, [bky])
                if hg % 2 == 1:
                    q4 = hg // 2
                    for i4 in range(4):
                        c = q4 * 4 + i4
                        self.stt(XY[c].ap[:, bs], XY[c].ap[:, bs], P["Dp"][:, j, c:c + 1], banky[:, i4 * 128:(i4 + 1) * 128], OP.mult, OP.add,
                                 XY[c].keys + [bky, "paramsB"], XY[c].keys)

            LA = 2
            for hg in range(8 + LA):
                if hg < 8:
                    stageA(hg)
                if hg >= LA:
                    stageB(hg - LA)
            if not samp:
                for g in range(4):
                    tmpS = tmpS_[g % 2]
                    bank, bk = self.ps()
                    self.mm(bank[:, :], BTOK.bf[:, g * 128:(g + 1) * 128], XW.bf[:, g * 512:(g + 1) * 512], True, True, BTOK.keys + XW.keys, [bk])
                    stg = ST[:, g * 512:(g + 1) * 512]
                    self.tt(tmpS.ap.rearrange("p (h d) -> p h d", h=8), stg.rearrange("p (h d) -> p h d", h=8),
                            elb[:, b, g * 8:(g + 1) * 8].unsqueeze(2).broadcast_to([128, 8, 64]), OP.mult, [("ST", j, g)] + tk.keys, tmpS.keys, eng=PEN)
                    self.tt(stg, tmpS.ap, bank[:, :], OP.add, tmpS.keys + [bk], [("ST", j, g)])
                    self.cp(STbf[:, g * 512:(g + 1) * 512], stg, [("ST", j, g)], [("STbf", g)], eng="act")
        XW, BTOK = XW_[0], BTOK_[0]
        if samp:
            if j == 0:
                self.dump("tk", tk.ap, tk.keys)
            self.ssd_sample_states(j, XY, CT4, XW, BTOK, small, c3q)
        elif tl == self.cfg.get("last", 3):
            self.q_i = mark_tmp
            stg = self.q(512)
            for q4 in range(4):
                bank, bk = self.ps()
                for i4 in range(4):
                    self.tr(bank[:, i4 * 128:(i4 + 1) * 128], ST[:, (q4 * 4 + i4) * 128:(q4 * 4 + i4 + 1) * 128], [("ST", j, q4)], [bk])
                self.cp(stg.ap, bank[:, :], [bk], stg.keys, eng="act")
                S.dma("sp", [(dr["ss_p"][j, q4 * 512:(q4 + 1) * 512, :].rearrange("(c p) n -> p c n", p=128),
                              stg.ap.rearrange("p (c n) -> p c n", c=4))], reads=stg.keys, semres=stg.keys[0], is_output=True)
            for hf in range(2):
                self.fm_to_tok(dr["sc_p"][j][:, hf * 1536:(hf + 1) * 1536], lambda c, hf=hf: self.ssmc[j][:, hf * 12 + c, :],
                               lambda c, hf=hf: [("ssmc", j, hf * 12 + c)], 3, 12, None)

    def ssd_sample_states(self, j, XY, CT4, XW, BTOK, small, c3q):
        S, dr, P = self.S, self.dr, self.P
        sm = small.ap
        cum_ = sm[0:96, 256:384]
        decq = self.q(256)
        decS = decq.ap.rearrange("p (c s) -> p c s", c=16)
        bank, bk = self.ps()
        for jc in range(16):
            for hh in range(2):
                self.mm(bank[hh * 64:(hh + 1) * 64, jc * 16:(jc + 1) * 16],
                        self.ident[0:32, 2 * jc + hh:2 * jc + hh + 1].broadcast_to([32, 64]),
                        cum_[0:32, 7:128:8], True, True, small.keys + ["consts"], [bk])
        self.act(decq.ap, bank[:, 0:256], AF.Exp, [bk], decq.keys)
        if j == 0:
            self.dump("decq", decq.ap, decq.keys)
            self.dump("small", small.ap, small.keys)
        ib = [self.ps(hold=True) for _ in range(4)]
        S0 = [self.q(2048) for _ in range(2)]
        S0T = [self.q(1024) for _ in range(2)]
        Sn = [self.q(2048) for _ in range(2)]
        Bm = [self.q(256) for _ in range(2)]
        rm = self.cst("rm16")
        xw3 = XW.bf[:, 0:2048]
        def load(sq):
            s0 = S0[sq % 2]
            s0v = s0.ap.rearrange("p (c n) -> p c n", c=16)
            S.dma("sp", [(s0v[:, hf * 8:(hf + 1) * 8, :], dr["st_ssm"][j, sq, hf * 1024:(hf + 1) * 1024, :].rearrange("(c p) n -> p c n", p=128))
                         for hf in range(2)], writes=s0.keys, semres=s0.keys[0])
        load(0)
        for sq in range(16):
            s0, s0t, sn, bm = S0[sq % 2], S0T[sq % 2], Sn[sq % 2], Bm[sq % 2]
            s0v = s0.ap.rearrange("p (c n) -> p c n", c=16)
            if sq + 1 < 16:
                load(sq + 1)
            for q4 in range(4):
                bank, bk = self.ps()
                for i4 in range(4):
                    self.tr(bank[:, i4 * 128:(i4 + 1) * 128], s0v[:, q4 * 4 + i4, :], s0.keys, [bk])
                self.cp(s0t.bf[:, q4 * 512:(q4 + 1) * 512], bank[:, :], [bk], s0t.keys, eng="act")
            for h in range(32):
                hg, i4 = h // 4, h % 4
                po = (h % 2) * 64
                col = ((h // 2) % 4) * 128 + sq * 8
                bnk, bkk = ib[h // 8]
                self.mm(bnk[po:po + 64, col:col + 8], s0t.bf[:, h * 64:(h + 1) * 64], CT4[hg].bf[:, i4 * 128 + sq * 8:i4 * 128 + sq * 8 + 8],
                        True, True, s0t.keys + CT4[hg].keys, [bkk])
            self.ts(bm.bf[:, 0:512], BTOK.bf[:, 0:512], rm[:, sq:sq + 1], None, OP.mult, None, BTOK.keys + ["consts"], bm.keys)
            snv = sn.ap.rearrange("p (c n) -> p c n", c=16)
            for q4 in range(4):
                bank, bk = self.ps()
                for i4 in range(4):
                    jc = q4 * 4 + i4
                    self.mm(bank[:, i4 * 128:(i4 + 1) * 128], xw3[:, jc * 128:(jc + 1) * 128], bm.bf[:, q4 * 128:(q4 + 1) * 128], True, True,
                            XW.keys + bm.keys, [bk])
                for i4 in range(4):
                    jc = q4 * 4 + i4
                    self.stt(snv[:, jc, :], s0v[:, jc, :], decS[:, jc, sq:sq + 1], bank[:, i4 * 128:(i4 + 1) * 128], OP.mult, OP.add,
                             s0.keys + decq.keys + [bk], sn.keys)
            S.dma("sp", [(dr["ss_s"][j, sq].rearrange("(c p) n -> p c n", p=128), snv)], reads=sn.keys, semres=sn.keys[0], is_output=True)
            if j == 0 and sq in (3, 8):
                self.dump("s0_%d" % sq, s0.ap, s0.keys)
                self.dump("sn_%d" % sq, sn.ap, sn.keys)
                self.dump("bm_%d" % sq, bm.bf[:, 0:512], bm.keys)
                self.dump("s0t_%d" % sq, s0t.bf[:, 0:2048], s0t.keys)
                if sq == 3:
                    self.dump("xw", XW.bf[:, 0:2048], XW.keys)
                    self.dump("btok", BTOK.bf[:, 0:512], BTOK.keys)
        for q4 in range(4):
            bnk, bkk = ib[q4]
            for i4 in range(4):
                c = q4 * 4 + i4
                self.tt(XY[c].ap[:, 0:128], XY[c].ap[:, 0:128], bnk[:, i4 * 128:(i4 + 1) * 128], OP.add, XY[c].keys + [bkk], XY[c].keys)
        self.ps_hold.clear()


class _NullSch:
    cnt = {e: 0 for e in ENGS}

    def op(self, *a, **k):
        return None

    def dma(self, *a, **k):
        return None


_CACHE = {}


def build_nc(cfg=None):
    cfg = cfg or {}
    b = Builder(cfg)
    nc = b.build()
    return nc, b


def shard_inputs(inp):
    consts = make_consts()
    maps = []
    for c in range(NCORES):
        sl = slice(16 * c, 16 * c + 16)
        m = {
            "xp": np.ascontiguousarray(inp["x_prompt"][c]),
            "xs": np.ascontiguousarray(inp["x_sample"][sl].reshape(128, D)),
            "pp": np.ascontiguousarray(inp["p_prompt"][:, c]),
            "psm": np.ascontiguousarray(inp["p_sample"][:, sl].reshape(4, 128, PLE)),
            "st_hgrn": np.ascontiguousarray(inp["state_hgrn"][:, sl]),
            "st_lru_h": np.ascontiguousarray(inp["state_lru_h"][:, sl]),
            "st_lru_conv": np.ascontiguousarray(inp["state_lru_conv"][:, sl].reshape(2, 48, 512)),
            "st_ssm": np.ascontiguousarray(inp["state_ssm"][:, sl].reshape(2, 16, 2048, 128)),
            "st_ssm_conv": np.ascontiguousarray(inp["state_ssm_conv"][:, sl].reshape(2, 48, 3072)),
            "consts": consts,
        }
        for n, _ in WNAMES:
            m[n] = np.ascontiguousarray(inp[n])
        maps.append(m)
    return maps


def gather_outputs(results):
    R = results
    cat = lambda n: [r[n] for r in R]
    y_prompt = np.stack(cat("yp"), 0)
    y_sample = np.concatenate([r["ys"].reshape(16, 8, D) for r in R], 0)
    hg_p = np.stack(cat("hg_p"), 1)
    hg_s = np.concatenate(cat("hg_s"), 1)
    lh_p = np.stack([r["lh_p"].reshape(2, 512) for r in R], 1)
    lh_s = np.concatenate(cat("lh_s"), 1)
    lc_p = np.stack(cat("lc_p"), 1)
    lc_s = np.concatenate([r["lc_s"].reshape(2, 16, 3, 512) for r in R], 1)
    ss_p = np.stack([r["ss_p"].reshape(2, 32, 64, 128) for r in R], 1)
    ss_s = np.concatenate([r["ss_s"].reshape(2, 16, 32, 64, 128) for r in R], 1)
    sc_p = np.stack(cat("sc_p"), 1)
    sc_s = np.concatenate([r["sc_s"].reshape(2, 16, 3, 3072) for r in R], 1)
    outs = (y_prompt, y_sample, hg_p, hg_s, lh_p, lh_s, lc_p, lc_s, ss_p, ss_s, sc_p, sc_s)
    return tuple(np.ascontiguousarray(o, dtype=np.float32) for o in outs)


def kernel(**inputs):
    inp = {k: np.asarray(v) for k, v in inputs.items()}
    nc, _ = build_nc()
    maps = shard_inputs(inp)
    res = run_bass_kernel_spmd(nc, maps, core_ids=list(range(NCORES)))
    return gather_outputs(res.results)
```
